# Optimizing a Trainium2 kernel written in Bass

```python
import jax, jax.numpy as jnp
from jax import lax
import numpy as np

D_MODEL = 2048
BATCH = 4
SEQ = 2048
DEPTH = 4
DEC_BATCH = 128
DEC_SEQ = 4
PAST_LEN = 16384
PAGE_SIZE = 128

N_EVEN = (DEPTH + 1) // 2
N_ODD = DEPTH // 2
POOL_WINDOWS = (2, 4, 8, 16)
POOL_GROUPS = len(POOL_WINDOWS)
A_WIDTH = D_MODEL // 2
POOL_GROUP_DIM = A_WIDTH // POOL_GROUPS
POOL_BUF = max(POOL_WINDOWS) - 1
B_WIDTH = D_MODEL // 2
SGU_HEADS = 4
SGU_HEAD_DIM = B_WIDTH // SGU_HEADS
SGU_CHUNK = 128
MIX_IN = A_WIDTH + 2 * B_WIDTH
MIX_OUT = A_WIDTH + B_WIDTH
RET_HEADS = 8
RET_DK = D_MODEL // RET_HEADS
RET_DV = 2 * RET_DK
RET_CHUNK = 128
ROPE_BASE = 10000.0
D_FF = 256 * ((8 * D_MODEL // 3 + 255) // 256)
CONV_W = 3
EPS = 1e-6

kernel_name = 'hybrid_pool_sgu_retention_decoder_step'

F32 = jnp.float32


def rms_norm(x, g):
    xf = x.astype(F32)
    y = xf * lax.rsqrt(jnp.mean(xf * xf, axis=-1, keepdims=True) + EPS)
    return (y * g.astype(F32)).astype(x.dtype)


def layer_norm(x, g, b=None):
    xf = x.astype(F32)
    mu = jnp.mean(xf, axis=-1, keepdims=True)
    xc = xf - mu
    y = xc * lax.rsqrt(jnp.mean(xc * xc, axis=-1, keepdims=True) + EPS) * g.astype(F32)
    if b is not None:
        y = y + b.astype(F32)
    return y


def rotary(x, pos):
    half = x.shape[-1] // 2
    inv = ROPE_BASE ** (-jnp.arange(half, dtype=F32) / half)
    ang = pos[:, None] * inv[None, :]
    cos = jnp.cos(ang)[None, :, None, :]
    sin = jnp.sin(ang)[None, :, None, :]
    xf = x.astype(F32)
    x1, x2 = xf[..., :half], xf[..., half:]
    return jnp.concatenate([x1 * cos - x2 * sin, x1 * sin + x2 * cos], axis=-1)


def pool_sgu_mixer(h, pool_buf, pos0, w_in, w_grp, pool_scale, w_s, b_s, sgu_g, sgu_b, w_out):
    bn, L, _ = h.shape
    z = h @ w_in
    a = z[..., :A_WIDTH]
    u = jax.nn.gelu(z[..., A_WIDTH:A_WIDTH + B_WIDTH], approximate=True)
    v = jax.nn.gelu(z[..., A_WIDTH + B_WIDTH:], approximate=True)
    ext = jnp.concatenate([pool_buf.astype(a.dtype), a], axis=1)
    cs = jnp.pad(jnp.cumsum(ext.astype(F32), axis=1), ((0, 0), (1, 0), (0, 0)))
    pos = pos0 + jnp.arange(L)
    end = POOL_BUF + 1
    pooled = []
    for gi, w in enumerate(POOL_WINDOWS):
        c0, c1 = gi * POOL_GROUP_DIM, (gi + 1) * POOL_GROUP_DIM
        s = cs[:, end:end + L, c0:c1] - cs[:, end - w:end - w + L, c0:c1]
        cnt = jnp.minimum(pos + 1, w).astype(F32)[None, :, None]
        pooled.append(s / cnt)
    pooled = jnp.concatenate(pooled, axis=-1)
    d = (pooled - a.astype(F32)).astype(a.dtype).reshape(bn, L, POOL_GROUPS, POOL_GROUP_DIM)
    a_out = jnp.einsum('blgc,gcd->blgd', d, w_grp).reshape(bn, L, A_WIDTH) * pool_scale
    new_pool = ext[:, -POOL_BUF:]
    vn = layer_norm(v, sgu_g, sgu_b).astype(v.dtype)
    C = min(L, SGU_CHUNK)
    n = L // C
    mask = jnp.tril(jnp.ones((C, C), dtype=bool))
    ws = jnp.where(mask, w_s[:, :C, :C], 0.0)
    vh = vn.reshape(bn, n, C, SGU_HEADS, SGU_HEAD_DIM)
    mixed = jnp.einsum('hij,bnjhd->bnihd', ws, vh) + b_s[:, :C].T[:, :, None]
    b_out = u * mixed.reshape(bn, L, B_WIDTH)
    y = jnp.concatenate([a_out, b_out], axis=-1) @ w_out
    return y, new_pool, vn


def retention_mixer(h, S0, pos0, w_q, w_k, w_v, w_g, gn_g, w_o):
    bn, L, _ = h.shape
    pos = (pos0 + jnp.arange(L)).astype(F32)
    q = rotary((h @ w_q).reshape(bn, L, RET_HEADS, RET_DK), pos)
    k = rotary((h @ w_k).reshape(bn, L, RET_HEADS, RET_DK), pos) * (RET_DK ** -0.5)
    v = (h @ w_v).reshape(bn, L, RET_HEADS, RET_DV).astype(F32)
    C = min(L, RET_CHUNK)
    n = L // C
    lg = jnp.log1p(-jnp.exp2(-5.0 - jnp.arange(RET_HEADS, dtype=F32)))
    idx = jnp.arange(C, dtype=F32)
    diff = idx[:, None] - idx[None, :]
    dmask = jnp.where(diff >= 0, jnp.exp(lg[:, None, None] * jnp.maximum(diff, 0.0)), 0.0)
    xi = jnp.exp(lg[:, None] * (idx + 1.0)).T
    zeta = jnp.exp(lg[:, None] * (C - 1.0 - idx)).T
    g_c = jnp.exp(lg * C)

    def step(S, blk):
        qc, kc, vc = blk
        sc = jnp.einsum('bihd,bjhd->bhij', qc, kc) * dmask
        o = jnp.einsum('bhij,bjhe->bihe', sc, vc) + jnp.einsum('bihd,bhde->bihe', qc, S) * xi[None, :, :, None]
        S = g_c[None, :, None, None] * S + jnp.einsum('bjhd,bjhe->bhde', kc * zeta[None, :, :, None], vc)
        return S, o

    def to_blocks(t):
        return t.reshape(bn, n, C, RET_HEADS, t.shape[-1]).swapaxes(0, 1)

    S_fin, o = lax.scan(step, S0.astype(F32), (to_blocks(q), to_blocks(k), to_blocks(v)))
    o = o.swapaxes(0, 1).reshape(bn, L, RET_HEADS, RET_DV)
    o = layer_norm(o, gn_g).reshape(bn, L, RET_HEADS * RET_DV).astype(h.dtype)
    y = (jax.nn.silu(h @ w_g) * o) @ w_o
    return y, S_fin.astype(S0.dtype)


def conv_ffn(h, buf, w_gate, w_up, w_conv, b_conv, w_down):
    L = h.shape[1]
    gate = h @ w_gate
    ext = jnp.concatenate([buf.astype(gate.dtype), gate], axis=1)
    conv = b_conv
    for i in range(CONV_W):
        conv = conv + ext[:, i:i + L] * w_conv[i]
    y = (jax.nn.gelu(conv, approximate=True) * (h @ w_up)) @ w_down
    return y, ext[:, -(CONV_W - 1):]


def setup_inputs(seed: int = 0) -> dict:
    key = jax.random.key(seed)
    ks = iter(jax.random.split(key, 40))

    def nrm(shape, scale):
        return jax.random.normal(next(ks), shape, F32) * scale

    def gain(shape):
        return 1.0 + 0.1 * jax.random.normal(next(ks), shape, F32)

    return {
        'x_prompt': nrm((BATCH, SEQ, D_MODEL), 1.0),
        'x_sample': nrm((DEC_BATCH, DEC_SEQ, D_MODEL), 1.0),
        'state_pool': nrm((N_EVEN, DEC_BATCH, POOL_BUF, A_WIDTH), 1.0),
        'state_ret': nrm((N_ODD, DEC_BATCH, RET_HEADS, RET_DK, RET_DV), 0.1),
        'state_conv': nrm((DEPTH, DEC_BATCH, CONV_W - 1, D_FF), 1.0),
        'w_mix_in': nrm((N_EVEN, D_MODEL, MIX_IN), D_MODEL ** -0.5),
        'w_pool_grp': nrm((N_EVEN, POOL_GROUPS, POOL_GROUP_DIM, POOL_GROUP_DIM), POOL_GROUP_DIM ** -0.5),
        'pool_scale': gain((N_EVEN, A_WIDTH)),
        'w_spatial': nrm((N_EVEN, SGU_HEADS, SGU_CHUNK, SGU_CHUNK), SGU_CHUNK ** -0.5),
        'b_spatial': gain((N_EVEN, SGU_HEADS, SGU_CHUNK)),
        'sgu_norm_g': gain((N_EVEN, B_WIDTH)),
        'sgu_norm_b': nrm((N_EVEN, B_WIDTH), 0.02),
        'w_mix_out': nrm((N_EVEN, MIX_OUT, D_MODEL), MIX_OUT ** -0.5),
        'w_q': nrm((N_ODD, D_MODEL, RET_HEADS * RET_DK), D_MODEL ** -0.5),
        'w_k': nrm((N_ODD, D_MODEL, RET_HEADS * RET_DK), D_MODEL ** -0.5),
        'w_v': nrm((N_ODD, D_MODEL, RET_HEADS * RET_DV), D_MODEL ** -0.5),
        'w_g': nrm((N_ODD, D_MODEL, RET_HEADS * RET_DV), D_MODEL ** -0.5),
        'ret_norm_g': gain((N_ODD, RET_HEADS, RET_DV)),
        'w_ret_out': nrm((N_ODD, RET_HEADS * RET_DV, D_MODEL), (RET_HEADS * RET_DV) ** -0.5),
        'norm_mix_pre': gain((DEPTH, D_MODEL)),
        'norm_mix_post': gain((DEPTH, D_MODEL)),
        'norm_ffn_pre': gain((DEPTH, D_MODEL)),
        'norm_ffn_post': gain((DEPTH, D_MODEL)),
        'w_ffn_gate': nrm((DEPTH, D_MODEL, D_FF), D_MODEL ** -0.5),
        'w_ffn_up': nrm((DEPTH, D_MODEL, D_FF), D_MODEL ** -0.5),
        'w_dconv': nrm((DEPTH, CONV_W, D_FF), CONV_W ** -0.5),
        'b_dconv': nrm((DEPTH, D_FF), 0.02),
        'w_ffn_down': nrm((DEPTH, D_FF, D_MODEL), D_FF ** -0.5),
    }


def reference(x_prompt, x_sample, state_pool, state_ret, state_conv,
              w_mix_in, w_pool_grp, pool_scale, w_spatial, b_spatial, sgu_norm_g, sgu_norm_b, w_mix_out,
              w_q, w_k, w_v, w_g, ret_norm_g, w_ret_out,
              norm_mix_pre, norm_mix_post, norm_ffn_pre, norm_ffn_post,
              w_ffn_gate, w_ffn_up, w_dconv, b_dconv, w_ffn_down):

    def run(x, pos0, pool_s, ret_s, conv_s):
        new_pool, new_v, new_ret, new_conv = [], [], [], []
        for l in range(DEPTH):
            h = rms_norm(x, norm_mix_pre[l])
            if l % 2 == 0:
                i = l // 2
                m, p_new, v_new = pool_sgu_mixer(h, pool_s[i], pos0, w_mix_in[i], w_pool_grp[i], pool_scale[i],
                                                 w_spatial[i], b_spatial[i], sgu_norm_g[i], sgu_norm_b[i],
                                                 w_mix_out[i])
                new_pool.append(p_new)
                new_v.append(v_new)
            else:
                j = l // 2
                m, s_new = retention_mixer(h, ret_s[j], pos0, w_q[j], w_k[j], w_v[j], w_g[j],
                                           ret_norm_g[j], w_ret_out[j])
                new_ret.append(s_new)
            x = x + rms_norm(m, norm_mix_post[l])
            h = rms_norm(x, norm_ffn_pre[l])
            f, c_new = conv_ffn(h, conv_s[l], w_ffn_gate[l], w_ffn_up[l], w_dconv[l], b_dconv[l], w_ffn_down[l])
            new_conv.append(c_new)
            x = x + rms_norm(f, norm_ffn_post[l])
        return x, jnp.stack(new_pool), jnp.stack(new_v), jnp.stack(new_ret), jnp.stack(new_conv)

    bp = x_prompt.shape[0]
    pool0 = jnp.zeros((N_EVEN, bp, POOL_BUF, A_WIDTH), x_prompt.dtype)
    ret0 = jnp.zeros((N_ODD, bp, RET_HEADS, RET_DK, RET_DV), state_ret.dtype)
    conv0 = jnp.zeros((DEPTH, bp, CONV_W - 1, D_FF), x_prompt.dtype)
    y_prompt, pool_prompt, _, ret_prompt, conv_prompt = run(x_prompt, 0, pool0, ret0, conv0)
    y_sample, pool_sample, chunk_v_sample, ret_sample, conv_sample = run(
        x_sample, PAST_LEN, state_pool, state_ret, state_conv)
    return (y_prompt, y_sample, pool_prompt, pool_sample, chunk_v_sample,
            ret_prompt, ret_sample, conv_prompt, conv_sample)
```

```python
import numpy as np
import ml_dtypes
import concourse.bass as bass
import concourse.mybir as mybir
from concourse.bass_utils import run_bass_kernel_spmd

F32 = mybir.dt.float32
BF16 = mybir.dt.bfloat16
AF = mybir.ActivationFunctionType
ALU = mybir.AluOpType

ENGS = ("pe", "act", "dve", "pool", "sp")
SEM_CHUNK = 3000
DMA_RING = 24


class Res:
    __slots__ = ("name", "w", "r")

    def __init__(self, name=""):
        self.name = name
        self.w = None
        self.r = []


class Op:
    __slots__ = ("eng", "fn", "deps", "sig", "dma", "sem", "val", "waits", "seq")

    def __init__(self, eng, fn, dma):
        self.eng = eng
        self.fn = fn
        self.dma = dma
        self.deps = []
        self.sig = False
        self.sem = None
        self.val = 0
        self.waits = []
        self.seq = -1


class Sched:
    def __init__(self, nc):
        self.nc = nc
        self.ops = {e: [] for e in ENGS}
        self.all_ops = []

    def add(self, eng, fn, reads=(), writes=(), dma=False, extra=()):
        op = Op(eng, fn, dma)
        deps = {}
        for o in extra:
            deps[id(o)] = o
        for r in reads:
            if r.w is not None:
                deps[id(r.w)] = r.w
        for w in writes:
            if w.w is not None:
                deps[id(w.w)] = w.w
            for o in w.r:
                deps[id(o)] = o
        dl = []
        for d in deps.values():
            if d.eng == "pe" and eng == "pe" and not d.dma and not dma:
                continue
            dl.append(d)
        op.deps = dl
        for r in reads:
            if not dma:
                r.r = [o for o in r.r if o.dma or o.eng != eng]
            r.r.append(op)
        for w in writes:
            w.w = op
            w.r = []
        op.seq = len(self.ops[eng])
        self.ops[eng].append(op)
        self.all_ops.append(op)
        return op

    def finalize_and_emit(self):
        nc = self.nc
        for op in self.all_ops:
            for d in op.deps:
                d.sig = True
        comp_sems = {e: [] for e in ENGS}
        dma_sems = {e: [] for e in ENGS}
        dma_prev = {}
        ncc = 0
        for e in ENGS:
            ncomp = 0
            ndma = 0
            slot_last = {}
            for op in self.ops[e]:
                if op.dma == "cc":
                    op.sig = True
                    op.sem = nc.alloc_semaphore(f"cc_{ncc}")
                    ncc += 1
                    op.val = 1
                elif op.dma:
                    op.sig = True
                    slot = ndma % DMA_RING
                    if slot >= len(dma_sems[e]):
                        dma_sems[e].append(nc.alloc_semaphore(f"d_{e}_{slot}"))
                    op.sem = dma_sems[e][slot]
                    op.val = 16 * (ndma // DMA_RING + 1)
                    if slot in slot_last:
                        dma_prev[id(op)] = slot_last[slot]
                    slot_last[slot] = op
                    ndma += 1
                elif op.sig:
                    ch = ncomp // SEM_CHUNK
                    if ch >= len(comp_sems[e]):
                        comp_sems[e].append(nc.alloc_semaphore(f"c_{e}_{ch}"))
                    op.sem = comp_sems[e][ch]
                    op.val = ncomp % SEM_CHUNK + 1
                    ncomp += 1
        for e in ENGS:
            waited = {}
            eng_seq = {}
            for op in self.ops[e]:
                ws = []
                deps = list(op.deps)
                if id(op) in dma_prev:
                    deps.append(dma_prev[id(op)])
                for d in deps:
                    if not d.dma:
                        if eng_seq.get(d.eng, -1) >= d.seq:
                            continue
                        eng_seq[d.eng] = d.seq
                    k = id(d.sem)
                    if waited.get(k, 0) >= d.val:
                        continue
                    waited[k] = d.val
                    ws.append((d.sem, d.val))
                op.waits = ws
        ops = self.ops

        def emit(engobj, lst):
            for op in lst:
                for (s, v) in op.waits:
                    engobj.wait_ge(s, v)
                ins = op.fn(engobj)
                if op.sig and ins is not None:
                    ins.then_inc(op.sem, 16 if op.dma is True else 1)

        with nc.Block() as block:
            @block.tensor
            def _(eng):
                emit(eng, ops["pe"])

            @block.scalar
            def _(eng):
                emit(eng, ops["act"])

            @block.vector
            def _(eng):
                emit(eng, ops["dve"])

            @block.gpsimd
            def _(eng):
                emit(eng, ops["pool"])

            @block.sync
            def _(eng):
                emit(eng, ops["sp"])


D = 2048
NT = 9
NTOK = NT * 128
AUX = 1024
FF = 5632
NFC = FF // 128
DEPTH = 4
EPS = 1e-6
PAST = 16384
HEADS = 8
GROUPS = [(0, 512), (512, 512), (1024, 128)]
WINS = (2, 4, 8, 16)


def _gammas():
    lg = np.log1p(-np.exp2(-5.0 - np.arange(HEADS, dtype=np.float32))).astype(np.float32)
    return lg


def make_consts(half):
    c = {}
    c["ident_bf"] = np.eye(128, dtype=np.float32).astype(ml_dtypes.bfloat16)
    c["ident_f"] = np.eye(128, dtype=np.float32)
    pos = np.zeros(NTOK, np.float32)
    pos[:1024] = half * 1024 + np.arange(1024)
    pos[AUX:AUX + 64] = PAST + (np.arange(64) // 16)
    inv = (np.float32(10000.0) ** (-(np.arange(128, dtype=np.float32)) / np.float32(128))).astype(np.float32)
    ang = (inv[:, None] * pos[None, :]).astype(np.float32)
    c["cos"] = np.cos(ang).astype(np.float32)
    c["sin"] = np.sin(ang).astype(np.float32)
    lg = _gammas().astype(np.float64)
    idx = np.arange(128)
    diff = idx[None, :] - idx[:, None]
    dm = np.zeros((128, HEADS, 128), np.float32)
    dma_ = np.zeros((128, HEADS, 128), np.float32)
    r = np.arange(128)
    tt = r // 16
    ss = r % 16
    valid = r < 64
    for h in range(HEADS):
        dm[:, h, :] = np.where(diff >= 0, np.exp(lg[h] * np.maximum(diff, 0)), 0.0)
        dt_ = tt[None, :] - tt[:, None]
        m = (ss[None, :] == ss[:, None]) & (dt_ >= 0) & valid[None, :] & valid[:, None]
        dma_[:, h, :] = np.where(m, np.exp(lg[h] * np.maximum(dt_, 0)), 0.0)
    c["dmask"] = dm
    c["dmask_aux"] = dma_
    xi = np.zeros((128, HEADS), np.float32)
    xia = np.zeros((128, HEADS), np.float32)
    zeta = np.zeros((128, HEADS), np.float32)
    zeta_a = np.zeros((128, HEADS), np.float32)
    for h in range(HEADS):
        xi[:, h] = np.exp(lg[h] * (idx + 1.0))
        xia[:, h] = np.where(valid, np.exp(lg[h] * (tt + 1.0)), 0.0)
        zeta[:, h] = np.exp(lg[h] * (127.0 - idx))
        zeta_a[:, h] = np.where(valid, np.exp(lg[h] * (3.0 - tt)), 0.0)
    zc = np.zeros((128, HEADS, 8), np.float32)
    for h in range(HEADS):
        for cc in range(8):
            zc[:, h, cc] = np.exp(lg[h] * (127.0 - idx + 128.0 * (7 - cc)))
    c["zeta_c"] = zc
    c["xi"] = xi
    c["xi_aux"] = xia
    c["zeta"] = zeta
    c["zeta_aux"] = zeta_a
    cm = np.zeros((128, 16, 64), np.float32)
    rm = np.zeros((128, 16), np.float32)
    for s in range(16):
        cm[:, s, :] = (np.arange(64) % 16 == s)[None, :]
        rm[:64, s] = (np.arange(64) % 16 == s)
    c["cmask"] = cm.astype(ml_dtypes.bfloat16)
    c["rmask"] = rm
    c["trimask"] = (idx[:, None] <= idx[None, :]).astype(np.float32)
    bd = np.zeros((64, 64), np.float32)
    r64 = np.arange(64)
    bd[:, :] = ((r64[:, None] % 16) == (r64[None, :] % 16)) & ((r64[:, None] // 16) <= (r64[None, :] // 16))
    bdm = np.zeros((128, 64), np.float32)
    bdm[:64] = bd
    c["bdmask"] = bdm
    E = np.zeros((128, 64), np.float32)
    for j in range(4):
        E[j, j * 16:(j + 1) * 16] = 1.0
    c["expand"] = E
    rc = np.zeros((128, 4, 16), np.float32)
    p16 = half * 1024 + np.arange(16)
    for g, w in enumerate(WINS):
        rc[:, g, :] = (1.0 / np.minimum(p16 + 1, w))[None, :]
    c["rc"] = rc
    c["flag"] = np.full((128, 1), float(half), np.float32)
    return c


CONST_SPECS = [
    ("ident_bf", [128, 128], BF16), ("ident_f", [128, 128], F32), ("cos", [128, NTOK], F32), ("sin", [128, NTOK], F32),
    ("dmask", [128, HEADS, 128], F32), ("dmask_aux", [128, HEADS, 128], F32), ("xi", [128, HEADS], F32),
    ("xi_aux", [128, HEADS], F32), ("zeta", [128, HEADS], F32), ("zeta_aux", [128, HEADS], F32), ("zeta_c", [128, HEADS, 8], F32),
    ("cmask", [128, 16, 64], BF16), ("rmask", [128, 16], F32), ("trimask", [128, 128], F32), ("bdmask", [128, 64], F32),
    ("expand", [128, 64], F32), ("rc", [128, 4, 16], F32), ("flag", [128, 1], F32),
]

WEIGHT_SPECS = [
    ("wt_a", [2, 8, 128, 2048]), ("wt_uv", [2, 4, 4, 128, 2048]), ("wt_grp", [2, 4, 128, 512]), ("pool_scale", [2, 1024]),
    ("w_spatial", [2, 4, 128, 128]), ("b_spatial", [2, 4, 128]), ("sgu_norm_g", [2, 1024]), ("sgu_norm_b", [2, 1024]),
    ("w_mix_out", [2, 2048, 2048]), ("wt_q", [2, 16, 128, 2048]), ("wt_k", [2, 16, 128, 2048]), ("wt_v", [2, 8, 4, 128, 2048]),
    ("wt_g", [2, 8, 4, 128, 2048]), ("ret_norm_g", [2, 8, 512]), ("w_ret_out", [2, 4096, 2048]),
    ("norm_mix_pre", [4, 2048]), ("norm_mix_post", [4, 2048]), ("norm_ffn_pre", [4, 2048]), ("norm_ffn_post", [4, 2048]),
    ("wt_gate", [4, 44, 128, 2048]), ("wt_up", [4, 44, 128, 2048]), ("w_dconv", [4, 3, 5632]), ("b_dconv", [4, 5632]),
    ("w_ffn_down", [4, 5632, 2048]),
]


def _tile_cols(w, ncol):
    L = w.shape[0]
    return np.ascontiguousarray(w.reshape(L, 16, 128, ncol, 128).transpose(0, 3, 2, 1, 4)).reshape(L, ncol, 128, 2048)


def _tile_k4(w, nblk):
    L = w.shape[0]
    return np.ascontiguousarray(w.reshape(L, 4, 4, 128, nblk, 512).transpose(0, 4, 1, 3, 2, 5)).reshape(L, nblk, 4, 128, 2048)


def tile_weights(inputs):
    f = lambda n: np.asarray(inputs[n], np.float32)
    out = {}
    win = f("w_mix_in")
    out["wt_a"] = _tile_cols(win[:, :, 0:1024], 8)
    out["wt_uv"] = _tile_k4(win[:, :, 1024:3072], 4)
    g = f("w_pool_grp")
    out["wt_grp"] = np.ascontiguousarray(g.reshape(2, 4, 2, 128, 256).transpose(0, 1, 3, 2, 4)).reshape(2, 4, 128, 512)
    out["wt_q"] = _tile_cols(f("w_q"), 16)
    out["wt_k"] = _tile_cols(f("w_k"), 16)
    out["wt_v"] = _tile_k4(f("w_v"), 8)
    out["wt_g"] = _tile_k4(f("w_g"), 8)
    out["wt_gate"] = _tile_cols(f("w_ffn_gate"), 44)
    out["wt_up"] = _tile_cols(f("w_ffn_up"), 44)
    for n, _ in WEIGHT_SPECS:
        if n not in out:
            out[n] = f(n)
    return out


def mm(out, lhsT, rhs, start, stop):
    return lambda e: e.matmul(out, lhsT=lhsT, rhs=rhs, start=start, stop=stop)


def tr(out, in_, ident):
    return lambda e: e.transpose(out=out, in_=in_, identity=ident)


def cp(out, in_):
    return lambda e: e.tensor_copy(out=out, in_=in_)


def actf(out, in_, func, scale=1.0, bias=0.0):
    return lambda e: e.activation(out=out, in_=in_, func=func, scale=scale, bias=bias)


def ts(out, in0, s1, s2, op0, op1=None):
    if op1 is None:
        return lambda e: e.tensor_scalar(out=out, in0=in0, scalar1=s1, scalar2=None, op0=op0)
    return lambda e: e.tensor_scalar(out=out, in0=in0, scalar1=s1, scalar2=s2, op0=op0, op1=op1)


def stt(out, in0, scalar, in1, op0, op1):
    return lambda e: e.scalar_tensor_tensor(out=out, in0=in0, scalar=scalar, in1=in1, op0=op0, op1=op1)


def tt(out, in0, in1, op):
    return lambda e: e.tensor_tensor(out=out, in0=in0, in1=in1, op=op)


def dma(out, in_):
    return lambda e: e.dma_start(out=out, in_=in_)


def dma_nc(out, in_):
    return lambda e: e.dma_start(out=out, in_=in_, allow_slow_non_contiguous=True)


def memset(ap, v):
    return lambda e: e.memset(ap, v)


class Prog:
    def __init__(self, n_layers=DEPTH):
        self.n_layers = n_layers
        nc = self.nc = bass.Bass("TRN2", target_bir_lowering=False)
        self.S = Sched(nc)
        din = lambda name, shape, dt=F32: nc.dram_tensor(name, shape, dt, kind="ExternalInput").ap()
        dout = lambda name, shape: nc.dram_tensor(name, shape, F32, kind="ExternalOutput").ap()
        self.x_in = din("x_in", [NTOK, D])
        self.st_pool = din("st_pool", [2, 16, 15, 1024])
        self.st_ret = din("st_ret", [2, 16, 8, 256, 512])
        self.st_conv = din("st_conv", [4, 16, 2, FF])
        self.W = {n: din(n, shp) for n, shp in WEIGHT_SPECS}
        self.C = {n: din("c_" + n, shp, dt) for n, shp, dt in CONST_SPECS}
        self.y_out = dout("y_out", [NTOK, D])
        self.pool_p = dout("pool_p", [2, 15, 1024])
        self.pool_s = dout("pool_s", [2, 16, 15, 1024])
        self.cv_s = dout("cv_s", [2, 64, 1024])
        self.ret_p = dout("ret_p", [2, 8, 256, 512])
        self.ret_s = dout("ret_s", [2, 16, 8, 256, 512])
        self.conv_p = dout("conv_p", [4, 2, FF])
        self.conv_s = dout("conv_s", [4, 16, 2, FF])
        self.x_cur = nc.dram_tensor("x_cur", [NTOK, D], F32).ap()
        self.xcurR = [Res() for _ in range(NT)]
        self.out_ops = []
        sb = nc.alloc_sbuf_tensor
        self.hT = sb("hT", [128, 16, NTOK], BF16)
        self.hT_R = [Res() for _ in range(NT)]
        self.yacc = sb("yacc", [128, NT, D], F32)
        self.y_R = [Res() for _ in range(NT)]
        self.x_from_in = True
        self.gp = sb("gp", [128, 16], F32)
        self.gpR = Res()
        self.NR = 6
        self.ring = sb("ring", [128, self.NR, 2048], BF16)
        self.ringR = [Res() for _ in range(self.NR)]
        self.ri = 0
        self.stats = [(sb(f"st6_{i}", [128, 4, 6], F32), sb(f"mv_{i}", [128, 2], F32), sb(f"t1_{i}", [128, 1], F32),
                       sb(f"rs_{i}", [128, 1], F32), Res()) for i in range(4)]
        self.stat_sub = {id(st[4]): [Res() for _ in range(4)] for st in self.stats}
        self.sti = 0
        self.ARENA = 12696
        self.arena = sb("arena", [128, self.ARENA], F32)
        self.atab = []
        self.banks = [(nc.alloc_psum_tensor(f"bank{i}", [128, 512], F32), Res()) for i in range(8)]
        self.bi = 0
        self.K = {}
        self.KR = {}
        for n, shp, dt in CONST_SPECS:
            self.K[n] = sb("k_" + n, shp, dt)
            self.KR[n] = Res()
            src = self.C[n]
            self.A("sp", dma(self.K[n][:], src), w=[self.KR[n]], dma=True)
        self.cc_n = 0

    def A(self, eng, fn, r=(), w=(), dma=False, extra=()):
        return self.S.add(eng, fn, reads=r, writes=w, dma=dma, extra=extra)

    def bank(self):
        b, r = self.banks[self.bi % 6]
        self.bi += 1
        return b, r

    def bank_ll(self, i=1):
        return self.banks[6 + i]

    def ring_unit(self):
        i = self.ri % self.NR
        self.ri += 1
        return self.ring[:, i, :], self.ringR[i]

    def stat(self):
        s = self.stats[self.sti % 4]
        self.sti += 1
        return s

    def abuf(self, off, n, dt=F32, shape=None):
        assert off + n <= self.ARENA, (off, n)
        R = Res()
        keep = []
        for (s, e, r0) in self.atab:
            if s < off + n and off < e:
                R.r.extend(r0.r)
                if r0.w is not None:
                    R.r.append(r0.w)
            else:
                keep.append((s, e, r0))
        keep.append((off, off + n, R))
        self.atab = keep
        ap = self.arena[:, off:off + n]
        if dt == BF16:
            ap = ap.bitcast(BF16)
        if shape is not None:
            names = " ".join(f"d{i}" for i in range(len(shape)))
            kw = {f"d{i}": shape[i] for i in range(len(shape))}
            ap = ap.rearrange(f"p ({names}) -> p {names}", **kw)
        return ap, R

    def wload(self, dst, src, R):
        return self.A("pool", dma(dst, src), w=[R], dma=True)

    def rms_stats(self, src_ap, srcR, eng="dve"):
        st6, mv, t1, rs, R = self.stat()
        sub = self.stat_sub[id(R)]
        for i in range(4):
            self.A("dve", (lambda e, i=i: e.bn_stats(out=st6[:, i, :], in_=src_ap[:, i * 512:(i + 1) * 512])), r=[srcR], w=[sub[i]])
        self.A("dve", lambda e: e.bn_aggr(out=mv[:], in_=st6[:].rearrange("p a b -> p (a b)")), r=sub + [R], w=[R])
        self.A("dve", stt(t1[:], mv[:, 0:1], mv[:, 0:1], mv[:, 1:2], ALU.mult, ALU.add), r=[R], w=[R])
        self.A("dve", ts(t1[:], t1[:], EPS, None, ALU.add), r=[R], w=[R])
        self.A("act", actf(t1[:], t1[:], AF.Sqrt), r=[R], w=[R])
        self.A("dve", lambda e: e.reciprocal(out=rs[:], in_=t1[:]), r=[R], w=[R])
        return rs, R

    def boundary(self, first, final, gpost, gpre):
        A = self.A
        K, KR = self.K, self.KR
        x_src = self.x_in if self.x_from_in else self.x_cur
        x_dst = self.y_out if final else self.x_cur
        self.xo, self.xoR = self.abuf(8600, 2048)
        self.gb, self.gbR = self.abuf(10648, 2048)
        self.hb, self.hbR = self.abuf(6552, 1024, BF16)
        if not first:
            A("sp", dma(self.gb[:], gpost.partition_broadcast(128)), w=[self.gbR], dma=True)
        if not final:
            A("sp", dma_nc(self.gp[:], gpre.rearrange("(k p) -> p k", p=128)), w=[self.gpR], dma=True)
        for t in range(NT):
            yt = self.yacc[:, t, :]
            yR = self.y_R[t]
            rows = slice(t * 128, (t + 1) * 128)
            if first:
                A("sp", dma(yt, self.x_in[rows, :]), w=[yR], dma=True)
            else:
                A("sp", dma(self.xo[:], x_src[rows, :]), r=[self.xcurR[t]], w=[self.xoR], dma=True)
                rs, sR = self.rms_stats(yt, yR)
                A("dve", stt(yt, yt, rs[:, 0:1], self.gb[:], ALU.mult, ALU.mult), r=[yR, sR, self.gbR], w=[yR])
                A("dve", tt(yt, yt, self.xo[:], ALU.add), r=[yR, self.xoR], w=[yR])
                if t == 7:
                    cin = self.nc.dram_tensor(f"cc_in{self.cc_n}", [32, D], F32).ap()
                    cout = self.nc.dram_tensor(f"cc_out{self.cc_n}", [64, D], F32).ap()
                    self.cc_n += 1
                    cR = Res()
                    A("sp", dma(cin, self.yacc[96:128, 7, :]), r=[yR], w=[cR], dma=True)
                    A("pool", lambda e, cin=cin, cout=cout: e.collective_compute(
                        "AllGather", ALU.bypass, replica_groups=[[0, 1], [2, 3], [4, 5], [6, 7]],
                        ins=[cin.opt()], outs=[cout.opt()]), r=[cR], w=[cR], dma="cc")
                    self.pend_halo = (cout, cR)
                if t == 8:
                    cout, cR = self.pend_halo
                    A("sp", dma(self.xo[64:96, :], cout[0:32, :]), r=[cR], w=[self.xoR], dma=True)
                    A("dve", ts(self.yacc[64:96, 8, :], self.xo[64:96, :], K["flag"][64:96, 0:1], None, ALU.mult),
                      r=[self.xoR, KR["flag"]], w=[yR])
                o = A("sp", dma(x_dst[rows, :], yt), r=[yR], w=[self.xcurR[t]], dma=True)
                if final:
                    self.out_ops.append(o)
            if final:
                continue
            rs, sR = self.rms_stats(yt, yR)
            A("act", actf(self.hb[:], yt, AF.Copy, scale=rs[:, 0:1]), r=[yR, sR], w=[self.hbR])
            for g in range(2):
                pb, pR = self.bank()
                pT = pb.bitcast(BF16).rearrange("p (k c) -> p k c", k=8)
                for j in range(8):
                    k = g * 8 + j
                    A("pe", tr(pT[:, j, :], self.hb[:, k * 128:(k + 1) * 128], K["ident_bf"][:]),
                      r=[self.hbR, KR["ident_bf"]], w=[pR])
                gpb = self.gp[:, g * 8:(g + 1) * 8].unsqueeze(2).to_broadcast([128, 8, 128])
                A("dve", tt(self.hT[:, g * 8:(g + 1) * 8, rows], pT, gpb, ALU.mult), r=[pR, self.gpR], w=[self.hT_R[t]])
        if not first:
            self.x_from_in = False

    def proj_out(self, first, act_aps, act_Rs, w_rows):
        A = self.A
        units = []
        for wr in w_rows:
            u, uR = self.ring_unit()
            self.wload(u, wr, uR)
            units.append((u, uR))
        n = len(units)
        for t in range(NT):
            for nb in range(4):
                pb, pR = self.bank()
                for ci, (u, uR) in enumerate(units):
                    A("pe", mm(pb[:], act_aps[ci][:, t * 128:(t + 1) * 128], u[:, nb * 512:(nb + 1) * 512], ci == 0, ci == n - 1),
                      r=[act_Rs[ci], uR], w=[pR])
                ysl = self.yacc[:, t, nb * 512:(nb + 1) * 512]
                if first:
                    A("act", actf(ysl, pb[:], AF.Copy), r=[pR], w=[self.y_R[t]])
                else:
                    A("dve", tt(ysl, ysl, pb[:], ALU.add), r=[pR, self.y_R[t]], w=[self.y_R[t]])

    def ffn(self, l):
        A = self.A
        K, KR = self.K, self.KR
        W = self.W
        hT = self.hT
        allh = list(self.hT_R)
        wd = W["w_ffn_down"][l]
        gsb, gsbR = self.abuf(0, 1156)
        cv, cvR = self.abuf(1156, 1152)
        ge, geR = self.abuf(2308, 1152)
        actT = []
        for i in range(2):
            a, r = self.abuf(3460 + i * 1152, 1152, BF16, shape=[2, NTOK])
            actT.append((a, r))
        es, esR = self.abuf(5764, 96)
        stc, stcR = self.abuf(5860, 128)
        cvo, cvoR = self.abuf(5988, 128)
        wc, wcR = self.abuf(6116, NFC * 3, shape=[3, NFC])
        bc, bcR = self.abuf(6116 + NFC * 3, NFC)
        upsb, upsbR = self.abuf(6400, 1152)
        for j in range(3):
            A("sp", dma_nc(wc[:, j, :], W["w_dconv"][l, j].rearrange("(c p) -> p c", p=128)), w=[wcR], dma=True)
        A("sp", dma_nc(bc, W["b_dconv"][l].rearrange("(c p) -> p c", p=128)), w=[bcR], dma=True)
        A("dve", memset(cv[:, AUX + 64:NTOK], 0.0), w=[cvR])
        pending = None
        for c in range(NFC):
            cs = slice(c * 128, (c + 1) * 128)
            slab_i = (c // 2) % 2
            aT, aR = actT[slab_i]
            ug, ugR = self.ring_unit()
            self.wload(ug, W["wt_gate"][l, c], ugR)
            uu, uuR = self.ring_unit()
            self.wload(uu, W["wt_up"][l, c], uuR)
            ugv = ug.rearrange("p (k n) -> p k n", k=16)
            uuv = uu.rearrange("p (k n) -> p k n", k=16)
            ups = []
            for (off, n) in GROUPS:
                hr = allh[off // 128:(off + n) // 128]
                pg, pgR = self.bank()
                for k in range(16):
                    A("pe", mm(pg[:, :n], ugv[:, k, :], hT[:, k, off:off + n], k == 0, k == 15), r=hr + [ugR], w=[pgR])
                A("act", actf(gsb[:, 2 + off:2 + off + n], pg[:, :n], AF.Copy), r=[pgR], w=[gsbR])
                pu, puR = self.bank()
                for k in range(16):
                    A("pe", mm(pu[:, :n], uuv[:, k, :], hT[:, k, off:off + n], k == 0, k == 15), r=hr + [uuR], w=[puR])
                A("act", actf(upsb[:, off:off + n], pu[:, :n], AF.Copy), r=[puR], w=[upsbR])
            if pending is not None:
                self.proj_out(*pending)
                pending = None
            A("dve", cp(gsb[:, 0:2], gsb[:, 2 + AUX + 94:2 + AUX + 96]), r=[gsbR], w=[gsbR])
            A("dve", cp(gsb[:, 2 + AUX + 100:2 + AUX + 102], gsb[:, 2 + 1022:2 + 1024]), r=[gsbR], w=[gsbR])
            A("dve", ts(cv[:, 0:1024], gsb[:, 0:1024], wc[:, 0, c:c + 1], bc[:, c:c + 1], ALU.mult, ALU.add),
              r=[gsbR, wcR, bcR], w=[cvR])
            A("dve", stt(cv[:, 0:1024], gsb[:, 1:1025], wc[:, 1, c:c + 1], cv[:, 0:1024], ALU.mult, ALU.add), r=[gsbR, cvR, wcR], w=[cvR])
            A("dve", stt(cv[:, 0:1024], gsb[:, 2:1026], wc[:, 2, c:c + 1], cv[:, 0:1024], ALU.mult, ALU.add), r=[gsbR, cvR, wcR], w=[cvR])
            for j in range(2):
                A("sp", dma(stc[j * 16:(j + 1) * 16, :], self.st_conv[l, :, j, cs]), w=[stcR], dma=True)
            pb, pR = self.bank()
            A("pe", tr(pb[:, 0:32], stc[0:32, :], K["ident_f"][0:32, 0:32]), r=[stcR, KR["ident_f"]], w=[pR])
            A("act", actf(es[:, 0:32], pb[:, 0:32], AF.Copy), r=[pR], w=[esR])
            A("act", actf(es[:, 32:96], gsb[:, 2 + AUX:2 + AUX + 64], AF.Copy), r=[gsbR], w=[esR])
            cva = cv[:, AUX:AUX + 64]
            A("dve", ts(cva, es[:, 0:64], wc[:, 0, c:c + 1], bc[:, c:c + 1], ALU.mult, ALU.add), r=[esR, wcR, bcR, cvR], w=[cvR])
            A("dve", stt(cva, es[:, 16:80], wc[:, 1, c:c + 1], cva, ALU.mult, ALU.add), r=[esR, cvR], w=[cvR])
            A("dve", stt(cva, es[:, 32:96], wc[:, 2, c:c + 1], cva, ALU.mult, ALU.add), r=[esR, cvR], w=[cvR])
            A("act", actf(ge[:, :], cv[:, :], AF.Gelu_apprx_tanh), r=[cvR], w=[geR])
            for gi, (off, n) in enumerate(GROUPS):
                A("dve", tt(aT[:, c % 2, off:off + n], ge[:, off:off + n], upsb[:, off:off + n], ALU.mult), r=[geR, upsbR], w=[aR])
            pb, pR = self.bank()
            A("pe", tr(pb[:, 0:128], gsb[:, 2 + AUX:2 + AUX + 128], K["ident_f"][:]), r=[gsbR, KR["ident_f"]], w=[pR])
            A("act", actf(cvo[:, :], pb[:, 0:128], AF.Copy), r=[pR], w=[cvoR])
            for j in range(2):
                self.out_ops.append(A("sp", dma(self.conv_s[l, :, j, cs], cvo[32 + j * 16:48 + j * 16, :]), r=[cvoR], dma=True))
            self.out_ops.append(A("sp", dma(self.conv_p[l, :, cs], cvo[100:102, :]), r=[cvoR], dma=True))
            if c % 2 == 1:
                s0 = c - 1
                pending = (s0 == 0, [aT[:, 0, :], aT[:, 1, :]], [aR, aR],
                           [wd[s0 * 128:(s0 + 1) * 128, :], wd[c * 128:(c + 1) * 128, :]])
        if pending is not None:
            self.proj_out(*pending)

    def even_mixer(self, l):
        i = l // 2
        A = self.A
        K, KR = self.K, self.KR
        W = self.W
        hT = self.hT
        allh = list(self.hT_R)
        wout = W["w_mix_out"][i]
        IDF, IDB = K["ident_f"], K["ident_bf"]
        boT, boTR = self.abuf(0, 4608, BF16, shape=[8, NTOK])
        vbufs = [self.abuf(5632, 1024), self.abuf(4608, 1024)]
        vnbufs = [self.abuf(6656, 512, BF16), self.abuf(8600, 512, BF16)]
        bobufs = [self.abuf(7168, 512, BF16), self.abuf(9112, 512, BF16)]
        wsT, wsTR = self.abuf(7680, 256, BF16, shape=[4, 128])
        wsf, wsfR = self.abuf(7936, 128)
        BD, BDR = self.abuf(8064, 128, BF16, shape=[4, 64])
        ysm, ysmR = self.abuf(8192, 64)
        ws4, ws4R = self.abuf(8256, 16, shape=[4, 4])
        bsT, bsTR = self.abuf(8272, 4)
        bsa, bsaR = self.abuf(8276, 4)
        bT4, bT4R = self.abuf(8280, 4)
        gbuf, gR = self.abuf(10648, 2048)
        A("sp", dma(gbuf[:, 0:1024], W["sgu_norm_g"][i].partition_broadcast(128)), w=[gR], dma=True)
        A("sp", dma(gbuf[:, 1024:2048], W["sgu_norm_b"][i].partition_broadcast(128)), w=[gR], dma=True)
        A("sp", dma_nc(bsT, W["b_spatial"][i].rearrange("h i -> i h")), w=[bsTR], dma=True)
        A("sp", dma_nc(bT4[0:4, :], W["b_spatial"][i, :, 0:4].rearrange("h i -> i h")), w=[bT4R], dma=True)
        for (bo_, boR_) in bobufs:
            A("dve", memset(bo_, 0.0), w=[boR_])
        for h in range(4):
            A("sp", dma(wsf, W["w_spatial"][i, h]), w=[wsfR], dma=True)
            pb, pR = self.bank()
            A("pe", tr(pb[:, 0:128], wsf, IDF[:]), r=[wsfR, KR["ident_f"]], w=[pR])
            A("dve", tt(wsT[:, h, :], pb[:, 0:128], K["trimask"][:], ALU.mult), r=[pR, KR["trimask"]], w=[wsTR])
            A("sp", dma(ws4[0:4, h, :], W["w_spatial"][i, h, 0:4, 0:4]), w=[ws4R], dma=True)
            pb, pR = self.bank()
            A("pe", mm(pb[0:4, 0:64], ws4[0:4, h, :], K["expand"][0:4, :], True, True), r=[ws4R, KR["expand"]], w=[pR])
            A("act", actf(ysm[0:4, :], pb[0:4, 0:64], AF.Copy), r=[pR], w=[ysmR])
            pb, pR = self.bank()
            A("pe", mm(pb[0:64, 0:64], K["expand"][0:4, :], ysm[0:4, :], True, True), r=[ysmR, KR["expand"]], w=[pR])
            A("dve", tt(BD[0:64, h, :], pb[0:64, 0:64], K["bdmask"][0:64, :], ALU.mult), r=[pR, KR["bdmask"]], w=[BDR])
        pb, pR = self.bank()
        A("pe", mm(pb[0:64, 0:4], K["expand"][0:4, :], bT4[0:4, :], True, True), r=[bT4R, KR["expand"]], w=[pR])
        A("act", actf(bsa[0:64, :], pb[0:64, 0:4], AF.Copy), r=[pR], w=[bsaR])
        for cb in range(4):
            cols = 1024 + cb * 512
            units = []
            for kk in range(4):
                u, uR = self.ring_unit()
                uv = u.rearrange("p (k n) -> p k n", k=4)
                self.wload(u, W["wt_uv"][i, cb, kk], uR)
                units.append((uv, uR))
            for t in range(NT):
                pb, pR = self.bank()
                for k in range(16):
                    uv, uR = units[k // 4]
                    A("pe", mm(pb[:], hT[:, k, t * 128:(t + 1) * 128], uv[:, k % 4, :], k == 0, k == 15), r=[allh[t], uR], w=[pR])
                A("act", actf(self.yacc[:, t, cb * 512:(cb + 1) * 512], pb[:], AF.Gelu_apprx_tanh), r=[pR], w=[self.y_R[t]])
        for t in range(NT):
            aux = (t == 8)
            P = slice(0, 64) if aux else slice(0, 128)
            yR = self.y_R[t]
            v_sb, vR = vbufs[t % 2]
            vnb, vnbR = vnbufs[t % 2]
            bo, boR = bobufs[t % 2]
            vsrc = self.yacc[:, t, 1024:2048]
            st6, mv, t1, rs, sR = self.stat()
            sub = self.stat_sub[id(sR)]
            for q in range(2):
                A("dve", (lambda e, q=q, st6=st6, vsrc=vsrc: e.bn_stats(out=st6[:, q, :], in_=vsrc[:, q * 512:(q + 1) * 512])), r=[yR], w=[sub[q]])
            A("dve", (lambda e, st6=st6, mv=mv: e.bn_aggr(out=mv[:], in_=st6[:, 0:2, :].rearrange("p a b -> p (a b)"))), r=sub[0:2] + [sR], w=[sR])
            A("dve", ts(t1[:], mv[:, 1:2], EPS, None, ALU.add), r=[sR], w=[sR])
            A("act", actf(t1[:], t1[:], AF.Sqrt), r=[sR], w=[sR])
            A("dve", (lambda e, rs=rs, t1=t1: e.reciprocal(out=rs[:], in_=t1[:])), r=[sR], w=[sR])
            A("dve", stt(t1[:], mv[:, 0:1], -1.0, rs[:], ALU.mult, ALU.mult), r=[sR], w=[sR])
            A("dve", ts(v_sb, vsrc, rs[:, 0:1], t1[:, 0:1], ALU.mult, ALU.add), r=[yR, sR], w=[vR])
            A("dve", tt(v_sb, v_sb, gbuf[:, 0:1024], ALU.mult), r=[vR, gR], w=[vR])
            if aux:
                A("dve", tt(v_sb, v_sb, gbuf[:, 1024:2048], ALU.add), r=[vR, gR], w=[vR])
                self.out_ops.append(A("sp", dma(self.cv_s[i], v_sb[0:64, :]), r=[vR], dma=True))
                A("act", actf(vnb, v_sb, AF.Copy), r=[vR], w=[vnbR])
            else:
                A("dve", tt(vnb, v_sb, gbuf[:, 1024:2048], ALU.add), r=[vR, gR], w=[vnbR])
            pbs = [self.bank(), self.bank()]
            for h in range(4):
                pb, pR = pbs[h // 2]
                osl = pb[P, (h % 2) * 256:(h % 2) * 256 + 256]
                if aux:
                    A("pe", mm(osl, BD[0:64, h, :], vnb[0:64, h * 256:(h + 1) * 256], True, True), r=[BDR, vnbR], w=[pR])
                    bias = bsa[0:64, h:h + 1]
                    bR = bsaR
                else:
                    A("pe", mm(osl, wsT[:, h, :], vnb[:, h * 256:(h + 1) * 256], True, True), r=[wsTR, vnbR], w=[pR])
                    bias = bsT[:, h:h + 1]
                    bR = bsTR
                A("dve", stt(bo[P, h * 256:(h + 1) * 256], osl, bias, self.yacc[P, t, h * 256:(h + 1) * 256], ALU.add, ALU.mult),
                  r=[pR, bR, yR], w=[boR])
            pb, pR = self.bank()
            pT = pb.bitcast(BF16).rearrange("p (k c) -> p k c", k=8)
            for m in range(8):
                A("pe", tr(pT[:, m, :], bo[:, m * 128:(m + 1) * 128], IDB[:]), r=[boR, KR["ident_bf"]], w=[pR])
            A("act", actf(boT[:, :, t * 128:(t + 1) * 128], pT, AF.Copy), r=[pR], w=[boTR])
        for sl in range(4):
            self.proj_out(sl == 0, [boT[:, 2 * sl, :], boT[:, 2 * sl + 1, :]], [boTR, boTR],
                          [wout[1024 + (2 * sl) * 128:1024 + (2 * sl + 1) * 128, :], wout[1024 + (2 * sl + 1) * 128:1024 + (2 * sl + 2) * 128, :]])
        asb, asbR = self.abuf(0, 1168)
        pa, paR = self.abuf(1168, 1168)
        pb_, pbR_ = self.abuf(2336, 1168)
        dT, dR = self.abuf(3504, 1152, BF16, shape=[2, NTOK])
        aoTs = [self.abuf(4656 + q * 1152, 1152, BF16, shape=[2, NTOK]) for q in range(2)]
        ext, extR = self.abuf(6960, 304)
        e1, e1R = self.abuf(7264, 304)
        e2, e2R = self.abuf(7568, 304)
        stp, stpR = self.abuf(7872, 256, shape=[2, 128])
        po, poR = self.abuf(8128, 128)
        psc, pscR = self.abuf(8256, 8)
        t16, t16R = self.abuf(8264, 16)
        pst, pstR = self.abuf(8280, 128)
        A("sp", dma_nc(psc, W["pool_scale"][i].rearrange("(j p) -> p j", p=128)), w=[pscR], dma=True)
        A("dve", memset(dT[:, :, AUX + 64:NTOK], 0.0), w=[dR])
        A("dve", memset(pst, 0.0), w=[pstR])
        self.out_ops.append(A("sp", dma(self.pool_s[i, :, 0:11, :], self.st_pool[i, :, 4:15, :]), dma=True))
        for j in range(8):
            g = j // 2
            w = WINS[g]
            fs = slice(j * 128, (j + 1) * 128)
            u, uR = self.ring_unit()
            uv = u.rearrange("p (k n) -> p k n", k=16)
            self.wload(u, W["wt_a"][i, j], uR)
            for (off, n) in GROUPS:
                hr = allh[off // 128:(off + n) // 128]
                pb, pR = self.bank()
                for k in range(16):
                    A("pe", mm(pb[:, :n], uv[:, k, :], hT[:, k, off:off + n], k == 0, k == 15), r=hr + [uR], w=[pR])
                A("act", actf(asb[:, 15 + off:15 + off + n], pb[:, :n], AF.Copy), r=[pR], w=[asbR])
            A("dve", cp(asb[:, 0:15], asb[:, 15 + AUX + 81:15 + AUX + 96]), r=[asbR], w=[asbR])
            Lp = 1039
            bufs = [(pa, paR), (pb_, pbR_)]
            src, srcR = asb, asbR
            sh = 1
            lv = 0
            while sh < w:
                dst, dstR = bufs[lv % 2]
                lo = 2 * sh - 1
                A("dve", tt(dst[:, lo:Lp], src[:, lo:Lp], src[:, lo - sh:Lp - sh], ALU.add), r=[srcR], w=[dstR])
                src, srcR = dst, dstR
                sh *= 2
                lv += 1
            A("dve", stt(dT[:, j % 2, 0:1024], src[:, 15:1039], 1.0 / w, asb[:, 15:1039], ALU.mult, ALU.subtract), r=[srcR, asbR], w=[dR])
            A("dve", tt(t16, src[:, 15:31], K["rc"][:, g, :], ALU.mult), r=[srcR, KR["rc"]], w=[t16R])
            A("dve", tt(dT[:, j % 2, 0:16], t16, asb[:, 15:31], ALU.subtract), r=[t16R, asbR, dR], w=[dR])
            for jt in range(15):
                A("sp", dma(stp[(jt % 8) * 16:(jt % 8) * 16 + 16, jt // 8, :], self.st_pool[i, :, jt, fs]), w=[stpR], dma=True)
            pb, pR = self.bank()
            A("pe", tr(pb[:, 0:128], stp[:, 0, :], IDF[:]), r=[stpR, KR["ident_f"]], w=[pR])
            A("pe", tr(pb[:, 128:240], stp[0:112, 1, :], IDF[0:112, 0:112]), r=[stpR, KR["ident_f"]], w=[pR])
            A("act", actf(ext[:, 0:240], pb[:, 0:240], AF.Copy), r=[pR], w=[extR])
            A("act", actf(ext[:, 240:304], asb[:, 15 + AUX:15 + AUX + 64], AF.Copy), r=[asbR], w=[extR])
            ebufs = [(e1, e1R), (e2, e2R)]
            src, srcR = ext, extR
            sh = 1
            lv = 0
            while sh < w:
                dst, dstR = ebufs[lv % 2]
                lo = (2 * sh - 1) * 16
                A("dve", tt(dst[:, lo:304], src[:, lo:304], src[:, lo - sh * 16:304 - sh * 16], ALU.add), r=[srcR], w=[dstR])
                src, srcR = dst, dstR
                sh *= 2
                lv += 1
            A("dve", stt(dT[:, j % 2, AUX:AUX + 64], src[:, 240:304], 1.0 / w, ext[:, 240:304], ALU.mult, ALU.subtract), r=[srcR, extR, dR], w=[dR])
            A("act", actf(pst[:, 0:64], asb[:, 15 + AUX:15 + AUX + 64], AF.Copy), r=[asbR], w=[pstR])
            A("act", actf(pst[:, 64:79], asb[:, 15 + 1009:15 + 1024], AF.Copy), r=[asbR], w=[pstR])
            pb, pR = self.bank()
            A("pe", tr(pb[:, 0:128], pst, IDF[:]), r=[pstR, KR["ident_f"]], w=[pR])
            A("act", actf(po, pb[:, 0:128], AF.Copy), r=[pR], w=[poR])
            for tq in range(4):
                self.out_ops.append(A("sp", dma(self.pool_s[i, :, 11 + tq, fs], po[tq * 16:(tq + 1) * 16, :]), r=[poR], dma=True))
            self.out_ops.append(A("sp", dma(self.pool_p[i, :, fs], po[64:79, :]), r=[poR], dma=True))
            if j % 2 == 1:
                aoT, aoR = aoTs[g % 2]
                u2, u2R = self.ring_unit()
                gv = u2[:, 0:512].rearrange("p (k n) -> p k n", k=2)
                self.wload(u2[:, 0:512], W["wt_grp"][i, g], u2R)
                for m in range(2):
                    for (off, n) in GROUPS:
                        pb, pR = self.bank()
                        for kc in range(2):
                            A("pe", mm(pb[:, :n], gv[:, kc, m * 128:(m + 1) * 128], dT[:, kc, off:off + n], kc == 0, kc == 1), r=[dR, u2R], w=[pR])
                        A("act", actf(aoT[:, m, off:off + n], pb[:, :n], AF.Copy, scale=psc[:, g * 2 + m:g * 2 + m + 1]), r=[pR, pscR], w=[aoR])
                self.proj_out(False, [aoT[:, 0, :], aoT[:, 1, :]], [aoR, aoR],
                              [wout[(2 * g) * 128:(2 * g + 1) * 128, :], wout[(2 * g + 1) * 128:(2 * g + 2) * 128, :]])

    def ret_mixer(self, l):
        jj = l // 2
        A = self.A
        K, KR = self.K, self.KR
        W = self.W
        hT = self.hT
        allh = list(self.hT_R)
        IDB = K["ident_bf"]
        wq, wk = W["wt_q"][jj], W["wt_k"][jj]
        wo = W["w_ret_out"][jj]
        lg = _gammas().astype(np.float64)
        v_tm, vR = self.abuf(0, 2304, BF16, shape=[9, 512])
        kT, kTR = self.abuf(2304, 1152, BF16, shape=[2, NTOK])
        qT, qTR = self.abuf(3456, 1152, BF16, shape=[2, NTOK])
        ogT, ogTR = self.abuf(4608, 2304, BF16, shape=[4, NTOK])
        S, SR = self.abuf(6912, 1024, shape=[2, 512])
        Sb, SbR = self.abuf(7936, 512, BF16, shape=[2, 512])
        scT, scTR = self.abuf(8448, 64, BF16)
        qm, qmR = self.abuf(8512, 32, BF16)
        raw, rawR = self.abuf(8600, 1024, shape=[2, 512])
        r1, r1R = self.abuf(9624, 512)
        r2, r2R = self.abuf(10136, 512)
        gnb, gnbR = self.abuf(10648, 512)
        o_sb, oR = self.abuf(11160, 512)
        gs, gsR = self.abuf(11672, 512)
        og, ogR = self.abuf(12184, 256, BF16)
        ktm, ktmR = self.abuf(12440, 128, BF16)
        kzm, kzmR = self.abuf(12568, 128, BF16)
        A("dve", memset(o_sb, 0.0), w=[oR])

        def kchunk_T(cols, zeta_ap, zR):
            pb, pR = self.bank()
            pT = pb.bitcast(BF16)
            for half in range(2):
                A("pe", tr(pT[:, half * 128:(half + 1) * 128], kT[:, half, cols], IDB[:]), r=[kTR, KR["ident_bf"]], w=[pR])
            A("dve", ts(ktm, pT[:, 0:256], zeta_ap, None, ALU.mult), r=[pR, zR], w=[ktmR])

        def finish_o(t, gunits):
            st6, mv, t1, rs, sR = self.stat()
            sub = self.stat_sub[id(sR)]
            A("dve", (lambda e, st6=st6: e.bn_stats(out=st6[:, 0, :], in_=o_sb)), r=[oR], w=[sub[0]])
            A("dve", (lambda e, st6=st6, mv=mv: e.bn_aggr(out=mv[:], in_=st6[:, 0, :])), r=[sub[0], sR], w=[sR])
            A("dve", ts(t1[:], mv[:, 1:2], EPS, None, ALU.add), r=[sR], w=[sR])
            A("act", actf(t1[:], t1[:], AF.Sqrt), r=[sR], w=[sR])
            A("dve", (lambda e, rs=rs, t1=t1: e.reciprocal(out=rs[:], in_=t1[:])), r=[sR], w=[sR])
            A("dve", ts(o_sb, o_sb, mv[:, 0:1], rs[:, 0:1], ALU.subtract, ALU.mult), r=[oR, sR], w=[oR])
            A("dve", tt(o_sb, o_sb, gnb, ALU.mult), r=[oR, gnbR], w=[oR])
            pb, pR = self.bank()
            for k in range(16):
                uv, uR = gunits[k // 4]
                A("pe", mm(pb[:], hT[:, k, t * 128:(t + 1) * 128], uv[:, k % 4, :], k == 0, k == 15), r=[allh[t], uR], w=[pR])
            A("act", actf(gs, pb[:], AF.Silu), r=[pR], w=[gsR])
            A("dve", tt(og, o_sb, gs, ALU.mult), r=[oR, gsR], w=[ogR])
            pb, pR = self.bank()
            pT = pb.bitcast(BF16).rearrange("p (k c) -> p k c", k=8)
            for m in range(4):
                A("pe", tr(pT[:, m, :], og[:, m * 128:(m + 1) * 128], IDB[:]), r=[ogR, KR["ident_bf"]], w=[pR])
            A("act", actf(ogT[:, :, t * 128:(t + 1) * 128], pT[:, 0:4, :], AF.Copy), r=[pR], w=[ogTR])

        def state_update(c, gc, first_zero):
            for half in range(2):
                pb, pR = self.bank()
                A("pe", mm(pb[:], ktm[:, half * 128:(half + 1) * 128], v_tm[:, c, :], True, True), r=[ktmR, vR], w=[pR])
                if first_zero:
                    A("act", actf(S[:, half, :], pb[:], AF.Copy), r=[pR], w=[SR])
                else:
                    A("dve", stt(S[:, half, :], S[:, half, :], gc, pb[:], ALU.mult, ALU.add), r=[pR, SR], w=[SR])

        for h in range(HEADS):
            gam128 = float(np.exp(lg[h] * 128.0))
            gam4 = float(np.exp(lg[h] * 4.0))
            for (wsrc, dst, dstR, scale) in ((wq, qT, qTR, 1.0), (wk, kT, kTR, 1.0 / 16.0)):
                units = []
                for half in range(2):
                    u, uR = self.ring_unit()
                    uv = u.rearrange("p (k n) -> p k n", k=16)
                    self.wload(u, wsrc[h * 2 + half], uR)
                    units.append((uv, uR))
                for (off, n) in GROUPS:
                    hr = allh[off // 128:(off + n) // 128]
                    for half in range(2):
                        uv, uR = units[half]
                        pb, pR = self.bank()
                        for k in range(16):
                            A("pe", mm(pb[:, :n], uv[:, k, :], hT[:, k, off:off + n], k == 0, k == 15), r=hr + [uR], w=[pR])
                        A("act", actf(raw[:, half, 0:n], pb[:, :n], AF.Copy, scale=scale), r=[pR], w=[rawR])
                    cs_, sn_ = K["cos"][:, off:off + n], K["sin"][:, off:off + n]
                    x1, x2 = raw[:, 0, 0:n], raw[:, 1, 0:n]
                    re_ = "dve"
                    A(re_, tt(r1[:, 0:n], x1, cs_, ALU.mult), r=[rawR, KR["cos"]], w=[r1R])
                    A(re_, tt(r2[:, 0:n], x2, sn_, ALU.mult), r=[rawR, KR["sin"]], w=[r2R])
                    A(re_, tt(dst[:, 0, off:off + n], r1[:, 0:n], r2[:, 0:n], ALU.subtract), r=[r1R, r2R], w=[dstR])
                    A(re_, tt(r1[:, 0:n], x1, sn_, ALU.mult), r=[rawR, KR["sin"]], w=[r1R])
                    A(re_, tt(r2[:, 0:n], x2, cs_, ALU.mult), r=[rawR, KR["cos"]], w=[r2R])
                    A(re_, tt(dst[:, 1, off:off + n], r1[:, 0:n], r2[:, 0:n], ALU.add), r=[r1R, r2R], w=[dstR])
            vunits = []
            for kk in range(4):
                u, uR = self.ring_unit()
                uv = u.rearrange("p (k n) -> p k n", k=4)
                self.wload(u, W["wt_v"][jj, h, kk], uR)
                vunits.append((uv, uR))
            for t in range(NT):
                pb, pR = self.bank()
                for k in range(16):
                    uv, uR = vunits[k // 4]
                    A("pe", mm(pb[:], hT[:, k, t * 128:(t + 1) * 128], uv[:, k % 4, :], k == 0, k == 15), r=[allh[t], uR], w=[pR])
                A("act", actf(v_tm[:, t, :], pb[:], AF.Copy), r=[pR], w=[vR])
            lls = [self.bank_ll(0), self.bank_ll(1)]
            for c in range(8):
                kchunk_T(slice(c * 128, (c + 1) * 128), K["zeta_c"][:, h, c:c + 1], KR["zeta_c"])
                for half in range(2):
                    lb, lR = lls[half]
                    A("pe", mm(lb[:], ktm[:, half * 128:(half + 1) * 128], v_tm[:, c, :], c == 0, c == 7), r=[ktmR, vR], w=[lR])
            for half in range(2):
                lb, lR = lls[half]
                A("act", actf(S[:, half, :], lb[:], AF.Copy), r=[lR], w=[SR])
            cin = self.nc.dram_tensor(f"ccs_in{jj}_{h}", [256, 512], F32).ap()
            cout = self.nc.dram_tensor(f"ccs_out{jj}_{h}", [512, 512], F32).ap()
            cR = Res()
            A("sp", dma(cin.rearrange("(k p) e -> p k e", p=128), S), r=[SR], w=[cR], dma=True)
            A("pool", lambda e, cin=cin, cout=cout: e.collective_compute(
                "AllGather", ALU.bypass, replica_groups=[[0, 1], [2, 3], [4, 5], [6, 7]],
                ins=[cin.opt()], outs=[cout.opt()]), r=[cR], w=[cR], dma="cc")
            gunits = []
            for kk in range(4):
                u, uR = self.ring_unit()
                uv = u.rearrange("p (k n) -> p k n", k=4)
                self.wload(u, W["wt_g"][jj, h, kk], uR)
                gunits.append((uv, uR))
            A("sp", dma(gnb, W["ret_norm_g"][jj, h].partition_broadcast(128)), w=[gnbR], dma=True)
            ac = slice(AUX, AUX + 128)
            pb, pR = self.bank()
            for half in range(2):
                A("pe", mm(pb[:, 0:128], kT[:, half, ac], qT[:, half, ac], half == 0, half == 1), r=[kTR, qTR], w=[pR])
            A("dve", tt(scT, pb[:, 0:128], K["dmask_aux"][:, h, :], ALU.mult), r=[pR, KR["dmask_aux"]], w=[scTR])
            pbi, pRi = self.bank()
            A("pe", mm(pbi[:], scT, v_tm[:, 8, :], True, True), r=[scTR, vR], w=[pRi])
            A("act", actf(o_sb, pbi[:], AF.Copy), r=[pRi], w=[oR])
            kchunk_T(ac, K["zeta_aux"][:, h:h + 1], KR["zeta_aux"])
            pbc, pRc = self.bank_ll()
            r12 = self.arena[:, 9624:10648].rearrange("p (k e) -> p k e", k=2)
            r12R = Res()
            r12R.r = [o for o in (r1R.r + r2R.r)] + [o for o in (r1R.w, r2R.w) if o is not None]
            stg = [(S, SR), (raw, rawR), (r12, r12R)]

            def load_state(s):
                b, bR = stg[s % 3]
                A("sp", dma(b, self.st_ret[jj, s, h].rearrange("(k p) e -> p k e", p=128)), w=[bR], dma=True)

            load_state(0)
            load_state(1)
            for s in range(16):
                Sx, SxR = stg[s % 3]
                A("act", actf(Sb, Sx, AF.Copy), r=[SxR], w=[SbR])
                for half in range(2):
                    A("dve", tt(qm, qT[:, half, AUX:AUX + 64], K["cmask"][:, s, :], ALU.mult), r=[qTR, KR["cmask"]], w=[qmR])
                    A("pe", mm(pbc[0:64, :], qm, Sb[:, half, :], (s == 0 and half == 0), (s == 15 and half == 1)), r=[qmR, SbR], w=[pRc])
                A("dve", ts(kzm, ktm, K["rmask"][:, s:s + 1], None, ALU.mult), r=[ktmR, KR["rmask"]], w=[kzmR])
                for half in range(2):
                    pb, pR = self.bank()
                    A("pe", mm(pb[:], kzm[:, half * 128:(half + 1) * 128], v_tm[:, 8, :], True, True), r=[kzmR, vR], w=[pR])
                    A("dve", stt(Sx[:, half, :], Sx[:, half, :], gam4, pb[:], ALU.mult, ALU.add), r=[pR, SxR], w=[SxR])
                self.out_ops.append(A("sp", dma(self.ret_s[jj, s, h].rearrange("(k p) e -> p k e", p=128), Sx), r=[SxR], dma=True))
                if s + 2 < 16:
                    load_state(s + 2)
            r1R.r = list(r12R.r) + ([r12R.w] if r12R.w is not None else [])
            r2R.r = list(r1R.r)
            r1R.w = None
            r2R.w = None
            A("dve", stt(o_sb[0:64, :], pbc[0:64, :], K["xi_aux"][0:64, h:h + 1], o_sb[0:64, :], ALU.mult, ALU.add), r=[pRc, KR["xi_aux"], oR], w=[oR])
            finish_o(8, gunits)
            A("sp", dma(S, cout[0:256, :].rearrange("(k p) e -> p k e", p=128)), r=[cR], w=[SR], dma=True)
            A("dve", ts(S, S, K["flag"][:, 0:1], None, ALU.mult), r=[SR, KR["flag"]], w=[SR])
            A("act", actf(Sb, S, AF.Copy), r=[SR], w=[SbR])
            for c in range(8):
                cc_ = slice(c * 128, (c + 1) * 128)
                pb, pR = self.bank()
                for half in range(2):
                    A("pe", mm(pb[:, 0:128], kT[:, half, cc_], qT[:, half, cc_], half == 0, half == 1), r=[kTR, qTR], w=[pR])
                A("dve", tt(scT, pb[:, 0:128], K["dmask"][:, h, :], ALU.mult), r=[pR, KR["dmask"]], w=[scTR])
                pbi, pRi = self.bank()
                A("pe", mm(pbi[:], scT, v_tm[:, c, :], True, True), r=[scTR, vR], w=[pRi])
                pbx, pRx = self.bank()
                for half in range(2):
                    A("pe", mm(pbx[:], qT[:, half, cc_], Sb[:, half, :], half == 0, half == 1), r=[qTR, SbR], w=[pRx])
                A("act", actf(o_sb, pbi[:], AF.Copy), r=[pRi], w=[oR])
                A("dve", stt(o_sb, pbx[:], K["xi"][:, h:h + 1], o_sb, ALU.mult, ALU.add), r=[pRx, KR["xi"], oR], w=[oR])
                kchunk_T(cc_, K["zeta"][:, h:h + 1], KR["zeta"])
                state_update(c, gam128, False)
                if c < 7:
                    A("act", actf(Sb, S, AF.Copy), r=[SR], w=[SbR])
                finish_o(c, gunits)
            self.out_ops.append(A("sp", dma(self.ret_p[jj, h].rearrange("(k p) e -> p k e", p=128), S), r=[SR], dma=True))
            for sl in range(2):
                r0 = h * 512 + sl * 256
                self.proj_out(h == 0 and sl == 0, [ogT[:, 2 * sl, :], ogT[:, 2 * sl + 1, :]], [ogTR, ogTR],
                              [wo[r0:r0 + 128, :], wo[r0 + 128:r0 + 256, :]])


    def build(self, mode="full"):
        gm_pre, gm_post = self.W["norm_mix_pre"], self.W["norm_mix_post"]
        gf_pre, gf_post = self.W["norm_ffn_pre"], self.W["norm_ffn_post"]
        L = self.n_layers
        if mode == "even_only":
            self.boundary(True, False, None, gm_pre[0])
            self.even_mixer(0)
            self.boundary(False, True, gm_post[0], None)
        elif mode == "ffn_only":
            self.boundary(True, False, None, gf_pre[0])
            self.ffn(0)
            self.boundary(False, True, gf_post[0], None)
        else:
            self.boundary(True, False, None, gm_pre[0])
            for l in range(L):
                if l % 2 == 0:
                    self.even_mixer(l)
                else:
                    self.ret_mixer(l)
                self.boundary(False, False, gm_post[l], gf_pre[l])
                self.ffn(l)
                last = (l == L - 1)
                self.boundary(False, last, gf_post[l], None if last else gm_pre[l + 1])
        self.A("sp", lambda e: None, extra=self.out_ops)
        self.S.finalize_and_emit()
        return self.nc


_PROG_CACHE = {}


def _get_prog(mode="full", n_layers=DEPTH):
    key = (mode, n_layers)
    if key not in _PROG_CACHE:
        p = Prog(n_layers)
        p.build(mode)
        _PROG_CACHE[key] = p.nc
    return _PROG_CACHE[key]


def make_in_maps(inputs):
    xp = np.asarray(inputs["x_prompt"], np.float32)
    xs = np.asarray(inputs["x_sample"], np.float32)
    maps = []
    wts = tile_weights(inputs)
    for c in range(8):
        p, half = c // 2, c % 2
        m = {}
        xin = np.zeros((NTOK, D), np.float32)
        xin[:1024] = xp[p, half * 1024:(half + 1) * 1024]
        xin[AUX:AUX + 64] = xs[16 * c:16 * c + 16].transpose(1, 0, 2).reshape(64, D)
        if half == 1:
            xin[AUX + 64:AUX + 96] = xp[p, 992:1024]
        m["x_in"] = xin
        m["st_pool"] = np.ascontiguousarray(inputs["state_pool"][:, 16 * c:16 * c + 16])
        m["st_ret"] = np.ascontiguousarray(inputs["state_ret"][:, 16 * c:16 * c + 16])
        m["st_conv"] = np.ascontiguousarray(inputs["state_conv"][:, 16 * c:16 * c + 16])
        for n, _ in WEIGHT_SPECS:
            m[n] = wts[n]
        for n, v in make_consts(half).items():
            m["c_" + n] = v
        maps.append(m)
    return maps


def assemble(res):
    R = res.results
    y_prompt = np.zeros((4, 2048, D), np.float32)
    y_sample = np.zeros((128, 4, D), np.float32)
    pool_prompt = np.zeros((2, 4, 15, 1024), np.float32)
    pool_sample = np.zeros((2, 128, 15, 1024), np.float32)
    chunk_v = np.zeros((2, 128, 4, 1024), np.float32)
    ret_prompt = np.zeros((2, 4, 8, 256, 512), np.float32)
    ret_sample = np.zeros((2, 128, 8, 256, 512), np.float32)
    conv_prompt = np.zeros((4, 4, 2, FF), np.float32)
    conv_sample = np.zeros((4, 128, 2, FF), np.float32)
    for c in range(8):
        p, half = c // 2, c % 2
        o = R[c]
        y_prompt[p, half * 1024:(half + 1) * 1024] = o["y_out"][:1024]
        y_sample[16 * c:16 * c + 16] = o["y_out"][AUX:AUX + 64].reshape(4, 16, D).transpose(1, 0, 2)
        pool_sample[:, 16 * c:16 * c + 16] = o["pool_s"]
        chunk_v[:, 16 * c:16 * c + 16] = o["cv_s"].reshape(2, 4, 16, 1024).transpose(0, 2, 1, 3)
        ret_sample[:, 16 * c:16 * c + 16] = o["ret_s"]
        conv_sample[:, 16 * c:16 * c + 16] = o["conv_s"]
        if half == 1:
            pool_prompt[:, p] = o["pool_p"]
            ret_prompt[:, p] = o["ret_p"]
            conv_prompt[:, p] = o["conv_p"]
    return (y_prompt, y_sample, pool_prompt, pool_sample, chunk_v, ret_prompt, ret_sample, conv_prompt, conv_sample)


def kernel(**inputs):
    nc = _get_prog()
    maps = make_in_maps(inputs)
    res = run_bass_kernel_spmd(nc, maps, core_ids=list(range(8)))
    return assemble(res)
```

```python
import numpy as np
import ml_dtypes
import concourse.bass as bass
import concourse.mybir as mybir
from concourse.bass_utils import run_bass_kernel_spmd

F32 = mybir.dt.float32
BF16 = mybir.dt.bfloat16
AF = mybir.ActivationFunctionType
ALU = mybir.AluOpType

ENGS = ("pe", "act", "dve", "pool", "sp")
SEM_CHUNK = 3000
DMA_RING = 24


class Res:
    __slots__ = ("name", "w", "r")

    def __init__(self, name=""):
        self.name = name
        self.w = None
        self.r = []


class Op:
    __slots__ = ("eng", "fn", "deps", "sig", "dma", "sem", "val", "waits", "seq")

    def __init__(self, eng, fn, dma):
        self.eng = eng
        self.fn = fn
        self.dma = dma
        self.deps = []
        self.sig = False
        self.sem = None
        self.val = 0
        self.waits = []
        self.seq = -1


class Sched:
    def __init__(self, nc):
        self.nc = nc
        self.ops = {e: [] for e in ENGS}
        self.all_ops = []

    def add(self, eng, fn, reads=(), writes=(), dma=False, extra=()):
        op = Op(eng, fn, dma)
        deps = {}
        for o in extra:
            deps[id(o)] = o
        for r in reads:
            if r.w is not None:
                deps[id(r.w)] = r.w
        for w in writes:
            if w.w is not None:
                deps[id(w.w)] = w.w
            for o in w.r:
                deps[id(o)] = o
        dl = []
        for d in deps.values():
            if d.eng == "pe" and eng == "pe" and not d.dma and not dma:
                continue
            dl.append(d)
        op.deps = dl
        for r in reads:
            if not dma:
                r.r = [o for o in r.r if o.dma or o.eng != eng]
            r.r.append(op)
        for w in writes:
            w.w = op
            w.r = []
        op.seq = len(self.ops[eng])
        self.ops[eng].append(op)
        self.all_ops.append(op)
        return op

    def finalize_and_emit(self):
        nc = self.nc
        for op in self.all_ops:
            for d in op.deps:
                d.sig = True
        comp_sems = {e: [] for e in ENGS}
        dma_sems = {e: [] for e in ENGS}
        dma_prev = {}
        ncc = 0
        for e in ENGS:
            ncomp = 0
            ndma = 0
            slot_last = {}
            for op in self.ops[e]:
                if op.dma == "cc":
                    op.sig = True
                    op.sem = nc.alloc_semaphore(f"cc_{ncc}")
                    ncc += 1
                    op.val = 1
                elif op.dma:
                    op.sig = True
                    slot = ndma % DMA_RING
                    if slot >= len(dma_sems[e]):
                        dma_sems[e].append(nc.alloc_semaphore(f"d_{e}_{slot}"))
                    op.sem = dma_sems[e][slot]
                    op.val = 16 * (ndma // DMA_RING + 1)
                    if slot in slot_last:
                        dma_prev[id(op)] = slot_last[slot]
                    slot_last[slot] = op
                    ndma += 1
                elif op.sig:
                    ch = ncomp // SEM_CHUNK
                    if ch >= len(comp_sems[e]):
                        comp_sems[e].append(nc.alloc_semaphore(f"c_{e}_{ch}"))
                    op.sem = comp_sems[e][ch]
                    op.val = ncomp % SEM_CHUNK + 1
                    ncomp += 1
        for e in ENGS:
            waited = {}
            eng_seq = {}
            for op in self.ops[e]:
                ws = []
                deps = list(op.deps)
                if id(op) in dma_prev:
                    deps.append(dma_prev[id(op)])
                for d in deps:
                    if not d.dma:
                        if eng_seq.get(d.eng, -1) >= d.seq:
                            continue
                        eng_seq[d.eng] = d.seq
                    k = id(d.sem)
                    if waited.get(k, 0) >= d.val:
                        continue
                    waited[k] = d.val
                    ws.append((d.sem, d.val))
                op.waits = ws
        ops = self.ops

        def emit(engobj, lst):
            for op in lst:
                for (s, v) in op.waits:
                    engobj.wait_ge(s, v)
                ins = op.fn(engobj)
                if op.sig and ins is not None:
                    ins.then_inc(op.sem, 16 if op.dma is True else 1)

        with nc.Block() as block:
            @block.tensor
            def _(eng):
                emit(eng, ops["pe"])

            @block.scalar
            def _(eng):
                emit(eng, ops["act"])

            @block.vector
            def _(eng):
                emit(eng, ops["dve"])

            @block.gpsimd
            def _(eng):
                emit(eng, ops["pool"])

            @block.sync
            def _(eng):
                emit(eng, ops["sp"])


D = 2048
NT = 9
NTOK = NT * 128
AUX = 1024
FF = 5632
NFC = FF // 128
DEPTH = 4
EPS = 1e-6
PAST = 16384
HEADS = 8
GROUPS = [(0, 512), (512, 512), (1024, 128)]
WINS = (2, 4, 8, 16)


def _gammas():
    lg = np.log1p(-np.exp2(-5.0 - np.arange(HEADS, dtype=np.float32))).astype(np.float32)
    return lg


def make_consts(half):
    c = {}
    c["ident_bf"] = np.eye(128, dtype=np.float32).astype(ml_dtypes.bfloat16)
    c["ident_f"] = np.eye(128, dtype=np.float32)
    pos = np.zeros(NTOK, np.float32)
    pos[:1024] = half * 1024 + np.arange(1024)
    pos[AUX:AUX + 64] = PAST + (np.arange(64) // 16)
    inv = (np.float32(10000.0) ** (-(np.arange(128, dtype=np.float32)) / np.float32(128))).astype(np.float32)
    ang = (inv[:, None] * pos[None, :]).astype(np.float32)
    c["cos"] = np.cos(ang).astype(np.float32)
    c["sin"] = np.sin(ang).astype(np.float32)
    lg = _gammas().astype(np.float64)
    idx = np.arange(128)
    diff = idx[None, :] - idx[:, None]
    dm = np.zeros((128, HEADS, 128), np.float32)
    dma_ = np.zeros((128, HEADS, 128), np.float32)
    r = np.arange(128)
    tt = r // 16
    ss = r % 16
    valid = r < 64
    for h in range(HEADS):
        dm[:, h, :] = np.where(diff >= 0, np.exp(lg[h] * np.maximum(diff, 0)), 0.0)
        dt_ = tt[None, :] - tt[:, None]
        m = (ss[None, :] == ss[:, None]) & (dt_ >= 0) & valid[None, :] & valid[:, None]
        dma_[:, h, :] = np.where(m, np.exp(lg[h] * np.maximum(dt_, 0)), 0.0)
    c["dmask"] = dm
    c["dmask_aux"] = dma_
    xi = np.zeros((128, HEADS), np.float32)
    xia = np.zeros((128, HEADS), np.float32)
    zeta = np.zeros((128, HEADS), np.float32)
    zeta_a = np.zeros((128, HEADS), np.float32)
    for h in range(HEADS):
        xi[:, h] = np.exp(lg[h] * (idx + 1.0))
        xia[:, h] = np.where(valid, np.exp(lg[h] * (tt + 1.0)), 0.0)
        zeta[:, h] = np.exp(lg[h] * (127.0 - idx))
        zeta_a[:, h] = np.where(valid, np.exp(lg[h] * (3.0 - tt)), 0.0)
    c["xi"] = xi
    c["xi_aux"] = xia
    c["zeta"] = zeta
    c["zeta_aux"] = zeta_a
    cm = np.zeros((128, 16, 64), np.float32)
    rm = np.zeros((128, 16), np.float32)
    for s in range(16):
        cm[:, s, :] = (np.arange(64) % 16 == s)[None, :]
        rm[:64, s] = (np.arange(64) % 16 == s)
    c["cmask"] = cm.astype(ml_dtypes.bfloat16)
    c["rmask"] = rm
    c["trimask"] = (idx[:, None] <= idx[None, :]).astype(np.float32)
    bd = np.zeros((64, 64), np.float32)
    r64 = np.arange(64)
    bd[:, :] = ((r64[:, None] % 16) == (r64[None, :] % 16)) & ((r64[:, None] // 16) <= (r64[None, :] // 16))
    bdm = np.zeros((128, 64), np.float32)
    bdm[:64] = bd
    c["bdmask"] = bdm
    E = np.zeros((128, 64), np.float32)
    for j in range(4):
        E[j, j * 16:(j + 1) * 16] = 1.0
    c["expand"] = E
    rc = np.zeros((128, 4, 16), np.float32)
    p16 = half * 1024 + np.arange(16)
    for g, w in enumerate(WINS):
        rc[:, g, :] = (1.0 / np.minimum(p16 + 1, w))[None, :]
    c["rc"] = rc
    c["flag"] = np.full((128, 1), float(half), np.float32)
    return c


CONST_SPECS = [
    ("ident_bf", [128, 128], BF16), ("ident_f", [128, 128], F32), ("cos", [128, NTOK], F32), ("sin", [128, NTOK], F32),
    ("dmask", [128, HEADS, 128], F32), ("dmask_aux", [128, HEADS, 128], F32), ("xi", [128, HEADS], F32),
    ("xi_aux", [128, HEADS], F32), ("zeta", [128, HEADS], F32), ("zeta_aux", [128, HEADS], F32),
    ("cmask", [128, 16, 64], BF16), ("rmask", [128, 16], F32), ("trimask", [128, 128], F32), ("bdmask", [128, 64], F32),
    ("expand", [128, 64], F32), ("rc", [128, 4, 16], F32), ("flag", [128, 1], F32),
]

WEIGHT_SPECS = [
    ("wt_a", [2, 8, 128, 2048]), ("wt_uv", [2, 4, 4, 128, 2048]), ("wt_grp", [2, 4, 128, 512]), ("pool_scale", [2, 1024]),
    ("w_spatial", [2, 4, 128, 128]), ("b_spatial", [2, 4, 128]), ("sgu_norm_g", [2, 1024]), ("sgu_norm_b", [2, 1024]),
    ("w_mix_out", [2, 2048, 2048]), ("wt_q", [2, 16, 128, 2048]), ("wt_k", [2, 16, 128, 2048]), ("wt_v", [2, 8, 4, 128, 2048]),
    ("wt_g", [2, 8, 4, 128, 2048]), ("ret_norm_g", [2, 8, 512]), ("w_ret_out", [2, 4096, 2048]),
    ("norm_mix_pre", [4, 2048]), ("norm_mix_post", [4, 2048]), ("norm_ffn_pre", [4, 2048]), ("norm_ffn_post", [4, 2048]),
    ("wt_gate", [4, 44, 128, 2048]), ("wt_up", [4, 44, 128, 2048]), ("w_dconv", [4, 3, 5632]), ("b_dconv", [4, 5632]),
    ("w_ffn_down", [4, 5632, 2048]),
]


def _tile_cols(w, ncol):
    L = w.shape[0]
    return np.ascontiguousarray(w.reshape(L, 16, 128, ncol, 128).transpose(0, 3, 2, 1, 4)).reshape(L, ncol, 128, 2048)


def _tile_k4(w, nblk):
    L = w.shape[0]
    return np.ascontiguousarray(w.reshape(L, 4, 4, 128, nblk, 512).transpose(0, 4, 1, 3, 2, 5)).reshape(L, nblk, 4, 128, 2048)


def tile_weights(inputs):
    f = lambda n: np.asarray(inputs[n], np.float32)
    out = {}
    win = f("w_mix_in")
    out["wt_a"] = _tile_cols(win[:, :, 0:1024], 8)
    out["wt_uv"] = _tile_k4(win[:, :, 1024:3072], 4)
    g = f("w_pool_grp")
    out["wt_grp"] = np.ascontiguousarray(g.reshape(2, 4, 2, 128, 256).transpose(0, 1, 3, 2, 4)).reshape(2, 4, 128, 512)
    out["wt_q"] = _tile_cols(f("w_q"), 16)
    out["wt_k"] = _tile_cols(f("w_k"), 16)
    out["wt_v"] = _tile_k4(f("w_v"), 8)
    out["wt_g"] = _tile_k4(f("w_g"), 8)
    out["wt_gate"] = _tile_cols(f("w_ffn_gate"), 44)
    out["wt_up"] = _tile_cols(f("w_ffn_up"), 44)
    for n, _ in WEIGHT_SPECS:
        if n not in out:
            out[n] = f(n)
    return out


def mm(out, lhsT, rhs, start, stop):
    return lambda e: e.matmul(out, lhsT=lhsT, rhs=rhs, start=start, stop=stop)


def tr(out, in_, ident):
    return lambda e: e.transpose(out=out, in_=in_, identity=ident)


def cp(out, in_):
    return lambda e: e.tensor_copy(out=out, in_=in_)


def actf(out, in_, func, scale=1.0, bias=0.0):
    return lambda e: e.activation(out=out, in_=in_, func=func, scale=scale, bias=bias)


def ts(out, in0, s1, s2, op0, op1=None):
    if op1 is None:
        return lambda e: e.tensor_scalar(out=out, in0=in0, scalar1=s1, scalar2=None, op0=op0)
    return lambda e: e.tensor_scalar(out=out, in0=in0, scalar1=s1, scalar2=s2, op0=op0, op1=op1)


def stt(out, in0, scalar, in1, op0, op1):
    return lambda e: e.scalar_tensor_tensor(out=out, in0=in0, scalar=scalar, in1=in1, op0=op0, op1=op1)


def tt(out, in0, in1, op):
    return lambda e: e.tensor_tensor(out=out, in0=in0, in1=in1, op=op)


def dma(out, in_):
    return lambda e: e.dma_start(out=out, in_=in_)


def dma_nc(out, in_):
    return lambda e: e.dma_start(out=out, in_=in_, allow_slow_non_contiguous=True)


def memset(ap, v):
    return lambda e: e.memset(ap, v)


class Prog:
    def __init__(self, n_layers=DEPTH):
        self.n_layers = n_layers
        nc = self.nc = bass.Bass("TRN2", target_bir_lowering=False)
        self.S = Sched(nc)
        din = lambda name, shape, dt=F32: nc.dram_tensor(name, shape, dt, kind="ExternalInput").ap()
        dout = lambda name, shape: nc.dram_tensor(name, shape, F32, kind="ExternalOutput").ap()
        self.x_in = din("x_in", [NTOK, D])
        self.st_pool = din("st_pool", [2, 16, 15, 1024])
        self.st_ret = din("st_ret", [2, 16, 8, 256, 512])
        self.st_conv = din("st_conv", [4, 16, 2, FF])
        self.W = {n: din(n, shp) for n, shp in WEIGHT_SPECS}
        self.C = {n: din("c_" + n, shp, dt) for n, shp, dt in CONST_SPECS}
        self.y_out = dout("y_out", [NTOK, D])
        self.pool_p = dout("pool_p", [2, 15, 1024])
        self.pool_s = dout("pool_s", [2, 16, 15, 1024])
        self.cv_s = dout("cv_s", [2, 64, 1024])
        self.ret_p = dout("ret_p", [2, 8, 256, 512])
        self.ret_s = dout("ret_s", [2, 16, 8, 256, 512])
        self.conv_p = dout("conv_p", [4, 2, FF])
        self.conv_s = dout("conv_s", [4, 16, 2, FF])
        self.x_cur = nc.dram_tensor("x_cur", [NTOK, D], F32).ap()
        self.xcurR = [Res() for _ in range(NT)]
        self.out_ops = []
        sb = nc.alloc_sbuf_tensor
        self.hT = sb("hT", [128, 16, NTOK], BF16)
        self.hT_R = [Res() for _ in range(NT)]
        self.yacc = sb("yacc", [128, NT, D], F32)
        self.y_R = [Res() for _ in range(NT)]
        self.x_from_in = True
        self.gp = sb("gp", [128, 16], F32)
        self.gpR = Res()
        self.NR = 6
        self.ring = sb("ring", [128, self.NR, 2048], BF16)
        self.ringR = [Res() for _ in range(self.NR)]
        self.ri = 0
        self.stats = [(sb(f"st6_{i}", [128, 4, 6], F32), sb(f"mv_{i}", [128, 2], F32), sb(f"t1_{i}", [128, 1], F32),
                       sb(f"rs_{i}", [128, 1], F32), Res()) for i in range(4)]
        self.stat_sub = {id(st[4]): [Res() for _ in range(4)] for st in self.stats}
        self.sti = 0
        self.ARENA = 12696
        self.arena = sb("arena", [128, self.ARENA], F32)
        self.atab = []
        self.banks = [(nc.alloc_psum_tensor(f"bank{i}", [128, 512], F32), Res()) for i in range(8)]
        self.bi = 0
        self.K = {}
        self.KR = {}
        for n, shp, dt in CONST_SPECS:
            self.K[n] = sb("k_" + n, shp, dt)
            self.KR[n] = Res()
            src = self.C[n]
            self.A("sp", dma(self.K[n][:], src), w=[self.KR[n]], dma=True)
        self.cc_n = 0

    def A(self, eng, fn, r=(), w=(), dma=False, extra=()):
        return self.S.add(eng, fn, reads=r, writes=w, dma=dma, extra=extra)

    def bank(self):
        b, r = self.banks[self.bi % 7]
        self.bi += 1
        return b, r

    def bank_ll(self):
        return self.banks[7]

    def ring_unit(self):
        i = self.ri % self.NR
        self.ri += 1
        return self.ring[:, i, :], self.ringR[i]

    def stat(self):
        s = self.stats[self.sti % 4]
        self.sti += 1
        return s

    def abuf(self, off, n, dt=F32, shape=None):
        assert off + n <= self.ARENA, (off, n)
        R = Res()
        keep = []
        for (s, e, r0) in self.atab:
            if s < off + n and off < e:
                R.r.extend(r0.r)
                if r0.w is not None:
                    R.r.append(r0.w)
            else:
                keep.append((s, e, r0))
        keep.append((off, off + n, R))
        self.atab = keep
        ap = self.arena[:, off:off + n]
        if dt == BF16:
            ap = ap.bitcast(BF16)
        if shape is not None:
            names = " ".join(f"d{i}" for i in range(len(shape)))
            kw = {f"d{i}": shape[i] for i in range(len(shape))}
            ap = ap.rearrange(f"p ({names}) -> p {names}", **kw)
        return ap, R

    def wload(self, dst, src, R):
        return self.A("pool", dma(dst, src), w=[R], dma=True)

    def rms_stats(self, src_ap, srcR, eng="dve"):
        st6, mv, t1, rs, R = self.stat()
        sub = self.stat_sub[id(R)]
        for i in range(4):
            self.A("dve", (lambda e, i=i: e.bn_stats(out=st6[:, i, :], in_=src_ap[:, i * 512:(i + 1) * 512])), r=[srcR], w=[sub[i]])
        self.A("dve", lambda e: e.bn_aggr(out=mv[:], in_=st6[:].rearrange("p a b -> p (a b)")), r=sub + [R], w=[R])
        self.A("dve", stt(t1[:], mv[:, 0:1], mv[:, 0:1], mv[:, 1:2], ALU.mult, ALU.add), r=[R], w=[R])
        self.A("dve", ts(t1[:], t1[:], EPS, None, ALU.add), r=[R], w=[R])
        self.A("act", actf(t1[:], t1[:], AF.Sqrt), r=[R], w=[R])
        self.A("dve", lambda e: e.reciprocal(out=rs[:], in_=t1[:]), r=[R], w=[R])
        return rs, R

    def boundary(self, first, final, gpost, gpre):
        A = self.A
        K, KR = self.K, self.KR
        x_src = self.x_in if self.x_from_in else self.x_cur
        x_dst = self.y_out if final else self.x_cur
        self.xo, self.xoR = self.abuf(8600, 2048)
        self.gb, self.gbR = self.abuf(10648, 2048)
        self.hb, self.hbR = self.abuf(6552, 1024, BF16)
        if not first:
            A("sp", dma(self.gb[:], gpost.partition_broadcast(128)), w=[self.gbR], dma=True)
        if not final:
            A("sp", dma_nc(self.gp[:], gpre.rearrange("(k p) -> p k", p=128)), w=[self.gpR], dma=True)
        for t in range(NT):
            yt = self.yacc[:, t, :]
            yR = self.y_R[t]
            rows = slice(t * 128, (t + 1) * 128)
            if first:
                A("sp", dma(yt, self.x_in[rows, :]), w=[yR], dma=True)
            else:
                A("sp", dma(self.xo[:], x_src[rows, :]), r=[self.xcurR[t]], w=[self.xoR], dma=True)
                rs, sR = self.rms_stats(yt, yR)
                A("dve", stt(yt, yt, rs[:, 0:1], self.gb[:], ALU.mult, ALU.mult), r=[yR, sR, self.gbR], w=[yR])
                A("dve", tt(yt, yt, self.xo[:], ALU.add), r=[yR, self.xoR], w=[yR])
                if t == 7:
                    cin = self.nc.dram_tensor(f"cc_in{self.cc_n}", [32, D], F32).ap()
                    cout = self.nc.dram_tensor(f"cc_out{self.cc_n}", [64, D], F32).ap()
                    self.cc_n += 1
                    cR = Res()
                    A("sp", dma(cin, self.yacc[96:128, 7, :]), r=[yR], w=[cR], dma=True)
                    A("pool", lambda e, cin=cin, cout=cout: e.collective_compute(
                        "AllGather", ALU.bypass, replica_groups=[[0, 1], [2, 3], [4, 5], [6, 7]],
                        ins=[cin.opt()], outs=[cout.opt()]), r=[cR], w=[cR], dma="cc")
                    self.pend_halo = (cout, cR)
                if t == 8:
                    cout, cR = self.pend_halo
                    A("sp", dma(self.xo[64:96, :], cout[0:32, :]), r=[cR], w=[self.xoR], dma=True)
                    A("dve", ts(self.yacc[64:96, 8, :], self.xo[64:96, :], K["flag"][64:96, 0:1], None, ALU.mult),
                      r=[self.xoR, KR["flag"]], w=[yR])
                o = A("sp", dma(x_dst[rows, :], yt), r=[yR], w=[self.xcurR[t]], dma=True)
                if final:
                    self.out_ops.append(o)
            if final:
                continue
            rs, sR = self.rms_stats(yt, yR)
            A("act", actf(self.hb[:], yt, AF.Copy, scale=rs[:, 0:1]), r=[yR, sR], w=[self.hbR])
            for g in range(2):
                pb, pR = self.bank()
                pT = pb.bitcast(BF16).rearrange("p (k c) -> p k c", k=8)
                for j in range(8):
                    k = g * 8 + j
                    A("pe", tr(pT[:, j, :], self.hb[:, k * 128:(k + 1) * 128], K["ident_bf"][:]),
                      r=[self.hbR, KR["ident_bf"]], w=[pR])
                gpb = self.gp[:, g * 8:(g + 1) * 8].unsqueeze(2).to_broadcast([128, 8, 128])
                A("dve", tt(self.hT[:, g * 8:(g + 1) * 8, rows], pT, gpb, ALU.mult), r=[pR, self.gpR], w=[self.hT_R[t]])
        if not first:
            self.x_from_in = False

    def proj_out(self, first, act_aps, act_Rs, w_rows):
        A = self.A
        units = []
        for wr in w_rows:
            u, uR = self.ring_unit()
            self.wload(u, wr, uR)
            units.append((u, uR))
        n = len(units)
        for t in range(NT):
            for nb in range(4):
                pb, pR = self.bank()
                for ci, (u, uR) in enumerate(units):
                    A("pe", mm(pb[:], act_aps[ci][:, t * 128:(t + 1) * 128], u[:, nb * 512:(nb + 1) * 512], ci == 0, ci == n - 1),
                      r=[act_Rs[ci], uR], w=[pR])
                ysl = self.yacc[:, t, nb * 512:(nb + 1) * 512]
                if first:
                    A("act", actf(ysl, pb[:], AF.Copy), r=[pR], w=[self.y_R[t]])
                else:
                    A("dve", tt(ysl, ysl, pb[:], ALU.add), r=[pR, self.y_R[t]], w=[self.y_R[t]])

    def ffn(self, l):
        A = self.A
        K, KR = self.K, self.KR
        W = self.W
        hT = self.hT
        allh = list(self.hT_R)
        wd = W["w_ffn_down"][l]
        gsb, gsbR = self.abuf(0, 1156)
        cv, cvR = self.abuf(1156, 1152)
        ge, geR = self.abuf(2308, 1152)
        actT = []
        for i in range(2):
            a, r = self.abuf(3460 + i * 1152, 1152, BF16, shape=[2, NTOK])
            actT.append((a, r))
        es, esR = self.abuf(5764, 96)
        stc, stcR = self.abuf(5860, 128)
        cvo, cvoR = self.abuf(5988, 128)
        wc, wcR = self.abuf(6116, NFC * 3, shape=[3, NFC])
        bc, bcR = self.abuf(6116 + NFC * 3, NFC)
        upsb, upsbR = self.abuf(6400, 1152)
        for j in range(3):
            A("sp", dma_nc(wc[:, j, :], W["w_dconv"][l, j].rearrange("(c p) -> p c", p=128)), w=[wcR], dma=True)
        A("sp", dma_nc(bc, W["b_dconv"][l].rearrange("(c p) -> p c", p=128)), w=[bcR], dma=True)
        A("dve", memset(cv[:, AUX + 64:NTOK], 0.0), w=[cvR])
        pending = None
        for c in range(NFC):
            cs = slice(c * 128, (c + 1) * 128)
            slab_i = (c // 2) % 2
            aT, aR = actT[slab_i]
            ug, ugR = self.ring_unit()
            self.wload(ug, W["wt_gate"][l, c], ugR)
            uu, uuR = self.ring_unit()
            self.wload(uu, W["wt_up"][l, c], uuR)
            ugv = ug.rearrange("p (k n) -> p k n", k=16)
            uuv = uu.rearrange("p (k n) -> p k n", k=16)
            ups = []
            for (off, n) in GROUPS:
                hr = allh[off // 128:(off + n) // 128]
                pg, pgR = self.bank()
                for k in range(16):
                    A("pe", mm(pg[:, :n], ugv[:, k, :], hT[:, k, off:off + n], k == 0, k == 15), r=hr + [ugR], w=[pgR])
                A("act", actf(gsb[:, 2 + off:2 + off + n], pg[:, :n], AF.Copy), r=[pgR], w=[gsbR])
                pu, puR = self.bank()
                for k in range(16):
                    A("pe", mm(pu[:, :n], uuv[:, k, :], hT[:, k, off:off + n], k == 0, k == 15), r=hr + [uuR], w=[puR])
                A("act", actf(upsb[:, off:off + n], pu[:, :n], AF.Copy), r=[puR], w=[upsbR])
            if pending is not None:
                self.proj_out(*pending)
                pending = None
            A("dve", cp(gsb[:, 0:2], gsb[:, 2 + AUX + 94:2 + AUX + 96]), r=[gsbR], w=[gsbR])
            A("dve", cp(gsb[:, 2 + AUX + 100:2 + AUX + 102], gsb[:, 2 + 1022:2 + 1024]), r=[gsbR], w=[gsbR])
            A("dve", ts(cv[:, 0:1024], gsb[:, 0:1024], wc[:, 0, c:c + 1], bc[:, c:c + 1], ALU.mult, ALU.add),
              r=[gsbR, wcR, bcR], w=[cvR])
            A("dve", stt(cv[:, 0:1024], gsb[:, 1:1025], wc[:, 1, c:c + 1], cv[:, 0:1024], ALU.mult, ALU.add), r=[gsbR, cvR, wcR], w=[cvR])
            A("dve", stt(cv[:, 0:1024], gsb[:, 2:1026], wc[:, 2, c:c + 1], cv[:, 0:1024], ALU.mult, ALU.add), r=[gsbR, cvR, wcR], w=[cvR])
            for j in range(2):
                A("sp", dma(stc[j * 16:(j + 1) * 16, :], self.st_conv[l, :, j, cs]), w=[stcR], dma=True)
            pb, pR = self.bank()
            A("pe", tr(pb[:, 0:32], stc[0:32, :], K["ident_f"][0:32, 0:32]), r=[stcR, KR["ident_f"]], w=[pR])
            A("act", actf(es[:, 0:32], pb[:, 0:32], AF.Copy), r=[pR], w=[esR])
            A("act", actf(es[:, 32:96], gsb[:, 2 + AUX:2 + AUX + 64], AF.Copy), r=[gsbR], w=[esR])
            cva = cv[:, AUX:AUX + 64]
            A("dve", ts(cva, es[:, 0:64], wc[:, 0, c:c + 1], bc[:, c:c + 1], ALU.mult, ALU.add), r=[esR, wcR, bcR, cvR], w=[cvR])
            A("dve", stt(cva, es[:, 16:80], wc[:, 1, c:c + 1], cva, ALU.mult, ALU.add), r=[esR, cvR], w=[cvR])
            A("dve", stt(cva, es[:, 32:96], wc[:, 2, c:c + 1], cva, ALU.mult, ALU.add), r=[esR, cvR], w=[cvR])
            A("act", actf(ge[:, :], cv[:, :], AF.Gelu_apprx_tanh), r=[cvR], w=[geR])
            for gi, (off, n) in enumerate(GROUPS):
                A("dve", tt(aT[:, c % 2, off:off + n], ge[:, off:off + n], upsb[:, off:off + n], ALU.mult), r=[geR, upsbR], w=[aR])
            pb, pR = self.bank()
            A("pe", tr(pb[:, 0:128], gsb[:, 2 + AUX:2 + AUX + 128], K["ident_f"][:]), r=[gsbR, KR["ident_f"]], w=[pR])
            A("act", actf(cvo[:, :], pb[:, 0:128], AF.Copy), r=[pR], w=[cvoR])
            for j in range(2):
                self.out_ops.append(A("sp", dma(self.conv_s[l, :, j, cs], cvo[32 + j * 16:48 + j * 16, :]), r=[cvoR], dma=True))
            self.out_ops.append(A("sp", dma(self.conv_p[l, :, cs], cvo[100:102, :]), r=[cvoR], dma=True))
            if c % 2 == 1:
                s0 = c - 1
                pending = (s0 == 0, [aT[:, 0, :], aT[:, 1, :]], [aR, aR],
                           [wd[s0 * 128:(s0 + 1) * 128, :], wd[c * 128:(c + 1) * 128, :]])
        if pending is not None:
            self.proj_out(*pending)

    def even_mixer(self, l):
        i = l // 2
        A = self.A
        K, KR = self.K, self.KR
        W = self.W
        hT = self.hT
        allh = list(self.hT_R)
        wout = W["w_mix_out"][i]
        IDF, IDB = K["ident_f"], K["ident_bf"]
        boT, boTR = self.abuf(0, 4608, BF16, shape=[8, NTOK])
        vbufs = [self.abuf(5632, 1024), self.abuf(4608, 1024)]
        vnbufs = [self.abuf(6656, 512, BF16), self.abuf(8600, 512, BF16)]
        bobufs = [self.abuf(7168, 512, BF16), self.abuf(9112, 512, BF16)]
        wsT, wsTR = self.abuf(7680, 256, BF16, shape=[4, 128])
        wsf, wsfR = self.abuf(7936, 128)
        BD, BDR = self.abuf(8064, 128, BF16, shape=[4, 64])
        ysm, ysmR = self.abuf(8192, 64)
        ws4, ws4R = self.abuf(8256, 16, shape=[4, 4])
        bsT, bsTR = self.abuf(8272, 4)
        bsa, bsaR = self.abuf(8276, 4)
        bT4, bT4R = self.abuf(8280, 4)
        gbuf, gR = self.abuf(10648, 2048)
        A("sp", dma(gbuf[:, 0:1024], W["sgu_norm_g"][i].partition_broadcast(128)), w=[gR], dma=True)
        A("sp", dma(gbuf[:, 1024:2048], W["sgu_norm_b"][i].partition_broadcast(128)), w=[gR], dma=True)
        A("sp", dma_nc(bsT, W["b_spatial"][i].rearrange("h i -> i h")), w=[bsTR], dma=True)
        A("sp", dma_nc(bT4[0:4, :], W["b_spatial"][i, :, 0:4].rearrange("h i -> i h")), w=[bT4R], dma=True)
        for (bo_, boR_) in bobufs:
            A("dve", memset(bo_, 0.0), w=[boR_])
        for h in range(4):
            A("sp", dma(wsf, W["w_spatial"][i, h]), w=[wsfR], dma=True)
            pb, pR = self.bank()
            A("pe", tr(pb[:, 0:128], wsf, IDF[:]), r=[wsfR, KR["ident_f"]], w=[pR])
            A("dve", tt(wsT[:, h, :], pb[:, 0:128], K["trimask"][:], ALU.mult), r=[pR, KR["trimask"]], w=[wsTR])
            A("sp", dma(ws4[0:4, h, :], W["w_spatial"][i, h, 0:4, 0:4]), w=[ws4R], dma=True)
            pb, pR = self.bank()
            A("pe", mm(pb[0:4, 0:64], ws4[0:4, h, :], K["expand"][0:4, :], True, True), r=[ws4R, KR["expand"]], w=[pR])
            A("act", actf(ysm[0:4, :], pb[0:4, 0:64], AF.Copy), r=[pR], w=[ysmR])
            pb, pR = self.bank()
            A("pe", mm(pb[0:64, 0:64], K["expand"][0:4, :], ysm[0:4, :], True, True), r=[ysmR, KR["expand"]], w=[pR])
            A("dve", tt(BD[0:64, h, :], pb[0:64, 0:64], K["bdmask"][0:64, :], ALU.mult), r=[pR, KR["bdmask"]], w=[BDR])
        pb, pR = self.bank()
        A("pe", mm(pb[0:64, 0:4], K["expand"][0:4, :], bT4[0:4, :], True, True), r=[bT4R, KR["expand"]], w=[pR])
        A("act", actf(bsa[0:64, :], pb[0:64, 0:4], AF.Copy), r=[pR], w=[bsaR])
        for cb in range(4):
            cols = 1024 + cb * 512
            units = []
            for kk in range(4):
                u, uR = self.ring_unit()
                uv = u.rearrange("p (k n) -> p k n", k=4)
                self.wload(u, W["wt_uv"][i, cb, kk], uR)
                units.append((uv, uR))
            for t in range(NT):
                pb, pR = self.bank()
                for k in range(16):
                    uv, uR = units[k // 4]
                    A("pe", mm(pb[:], hT[:, k, t * 128:(t + 1) * 128], uv[:, k % 4, :], k == 0, k == 15), r=[allh[t], uR], w=[pR])
                A("act", actf(self.yacc[:, t, cb * 512:(cb + 1) * 512], pb[:], AF.Gelu_apprx_tanh), r=[pR], w=[self.y_R[t]])
        for t in range(NT):
            aux = (t == 8)
            P = slice(0, 64) if aux else slice(0, 128)
            yR = self.y_R[t]
            v_sb, vR = vbufs[t % 2]
            vnb, vnbR = vnbufs[t % 2]
            bo, boR = bobufs[t % 2]
            vsrc = self.yacc[:, t, 1024:2048]
            st6, mv, t1, rs, sR = self.stat()
            sub = self.stat_sub[id(sR)]
            for q in range(2):
                A("dve", (lambda e, q=q, st6=st6, vsrc=vsrc: e.bn_stats(out=st6[:, q, :], in_=vsrc[:, q * 512:(q + 1) * 512])), r=[yR], w=[sub[q]])
            A("dve", (lambda e, st6=st6, mv=mv: e.bn_aggr(out=mv[:], in_=st6[:, 0:2, :].rearrange("p a b -> p (a b)"))), r=sub[0:2] + [sR], w=[sR])
            A("dve", ts(t1[:], mv[:, 1:2], EPS, None, ALU.add), r=[sR], w=[sR])
            A("act", actf(t1[:], t1[:], AF.Sqrt), r=[sR], w=[sR])
            A("dve", (lambda e, rs=rs, t1=t1: e.reciprocal(out=rs[:], in_=t1[:])), r=[sR], w=[sR])
            A("dve", stt(t1[:], mv[:, 0:1], -1.0, rs[:], ALU.mult, ALU.mult), r=[sR], w=[sR])
            A("dve", ts(v_sb, vsrc, rs[:, 0:1], t1[:, 0:1], ALU.mult, ALU.add), r=[yR, sR], w=[vR])
            A("dve", tt(v_sb, v_sb, gbuf[:, 0:1024], ALU.mult), r=[vR, gR], w=[vR])
            if aux:
                A("dve", tt(v_sb, v_sb, gbuf[:, 1024:2048], ALU.add), r=[vR, gR], w=[vR])
                self.out_ops.append(A("sp", dma(self.cv_s[i], v_sb[0:64, :]), r=[vR], dma=True))
                A("act", actf(vnb, v_sb, AF.Copy), r=[vR], w=[vnbR])
            else:
                A("dve", tt(vnb, v_sb, gbuf[:, 1024:2048], ALU.add), r=[vR, gR], w=[vnbR])
            pbs = [self.bank(), self.bank()]
            for h in range(4):
                pb, pR = pbs[h // 2]
                osl = pb[P, (h % 2) * 256:(h % 2) * 256 + 256]
                if aux:
                    A("pe", mm(osl, BD[0:64, h, :], vnb[0:64, h * 256:(h + 1) * 256], True, True), r=[BDR, vnbR], w=[pR])
                    bias = bsa[0:64, h:h + 1]
                    bR = bsaR
                else:
                    A("pe", mm(osl, wsT[:, h, :], vnb[:, h * 256:(h + 1) * 256], True, True), r=[wsTR, vnbR], w=[pR])
                    bias = bsT[:, h:h + 1]
                    bR = bsTR
                A("dve", stt(bo[P, h * 256:(h + 1) * 256], osl, bias, self.yacc[P, t, h * 256:(h + 1) * 256], ALU.add, ALU.mult),
                  r=[pR, bR, yR], w=[boR])
            pb, pR = self.bank()
            pT = pb.bitcast(BF16).rearrange("p (k c) -> p k c", k=8)
            for m in range(8):
                A("pe", tr(pT[:, m, :], bo[:, m * 128:(m + 1) * 128], IDB[:]), r=[boR, KR["ident_bf"]], w=[pR])
            A("act", actf(boT[:, :, t * 128:(t + 1) * 128], pT, AF.Copy), r=[pR], w=[boTR])
        for sl in range(4):
            self.proj_out(sl == 0, [boT[:, 2 * sl, :], boT[:, 2 * sl + 1, :]], [boTR, boTR],
                          [wout[1024 + (2 * sl) * 128:1024 + (2 * sl + 1) * 128, :], wout[1024 + (2 * sl + 1) * 128:1024 + (2 * sl + 2) * 128, :]])
        asb, asbR = self.abuf(0, 1168)
        pa, paR = self.abuf(1168, 1168)
        pb_, pbR_ = self.abuf(2336, 1168)
        dT, dR = self.abuf(3504, 1152, BF16, shape=[2, NTOK])
        aoTs = [self.abuf(4656 + q * 1152, 1152, BF16, shape=[2, NTOK]) for q in range(2)]
        ext, extR = self.abuf(6960, 304)
        e1, e1R = self.abuf(7264, 304)
        e2, e2R = self.abuf(7568, 304)
        stp, stpR = self.abuf(7872, 256, shape=[2, 128])
        po, poR = self.abuf(8128, 128)
        psc, pscR = self.abuf(8256, 8)
        t16, t16R = self.abuf(8264, 16)
        pst, pstR = self.abuf(8280, 128)
        A("sp", dma_nc(psc, W["pool_scale"][i].rearrange("(j p) -> p j", p=128)), w=[pscR], dma=True)
        A("dve", memset(dT[:, :, AUX + 64:NTOK], 0.0), w=[dR])
        A("dve", memset(pst, 0.0), w=[pstR])
        self.out_ops.append(A("sp", dma(self.pool_s[i, :, 0:11, :], self.st_pool[i, :, 4:15, :]), dma=True))
        for j in range(8):
            g = j // 2
            w = WINS[g]
            fs = slice(j * 128, (j + 1) * 128)
            u, uR = self.ring_unit()
            uv = u.rearrange("p (k n) -> p k n", k=16)
            self.wload(u, W["wt_a"][i, j], uR)
            for (off, n) in GROUPS:
                hr = allh[off // 128:(off + n) // 128]
                pb, pR = self.bank()
                for k in range(16):
                    A("pe", mm(pb[:, :n], uv[:, k, :], hT[:, k, off:off + n], k == 0, k == 15), r=hr + [uR], w=[pR])
                A("act", actf(asb[:, 15 + off:15 + off + n], pb[:, :n], AF.Copy), r=[pR], w=[asbR])
            A("dve", cp(asb[:, 0:15], asb[:, 15 + AUX + 81:15 + AUX + 96]), r=[asbR], w=[asbR])
            Lp = 1039
            bufs = [(pa, paR), (pb_, pbR_)]
            src, srcR = asb, asbR
            sh = 1
            lv = 0
            while sh < w:
                dst, dstR = bufs[lv % 2]
                lo = 2 * sh - 1
                A("dve", tt(dst[:, lo:Lp], src[:, lo:Lp], src[:, lo - sh:Lp - sh], ALU.add), r=[srcR], w=[dstR])
                src, srcR = dst, dstR
                sh *= 2
                lv += 1
            A("dve", stt(dT[:, j % 2, 0:1024], src[:, 15:1039], 1.0 / w, asb[:, 15:1039], ALU.mult, ALU.subtract), r=[srcR, asbR], w=[dR])
            A("dve", tt(t16, src[:, 15:31], K["rc"][:, g, :], ALU.mult), r=[srcR, KR["rc"]], w=[t16R])
            A("dve", tt(dT[:, j % 2, 0:16], t16, asb[:, 15:31], ALU.subtract), r=[t16R, asbR, dR], w=[dR])
            for jt in range(15):
                A("sp", dma(stp[(jt % 8) * 16:(jt % 8) * 16 + 16, jt // 8, :], self.st_pool[i, :, jt, fs]), w=[stpR], dma=True)
            pb, pR = self.bank()
            A("pe", tr(pb[:, 0:128], stp[:, 0, :], IDF[:]), r=[stpR, KR["ident_f"]], w=[pR])
            A("pe", tr(pb[:, 128:240], stp[0:112, 1, :], IDF[0:112, 0:112]), r=[stpR, KR["ident_f"]], w=[pR])
            A("act", actf(ext[:, 0:240], pb[:, 0:240], AF.Copy), r=[pR], w=[extR])
            A("act", actf(ext[:, 240:304], asb[:, 15 + AUX:15 + AUX + 64], AF.Copy), r=[asbR], w=[extR])
            ebufs = [(e1, e1R), (e2, e2R)]
            src, srcR = ext, extR
            sh = 1
            lv = 0
            while sh < w:
                dst, dstR = ebufs[lv % 2]
                lo = (2 * sh - 1) * 16
                A("dve", tt(dst[:, lo:304], src[:, lo:304], src[:, lo - sh * 16:304 - sh * 16], ALU.add), r=[srcR], w=[dstR])
                src, srcR = dst, dstR
                sh *= 2
                lv += 1
            A("dve", stt(dT[:, j % 2, AUX:AUX + 64], src[:, 240:304], 1.0 / w, ext[:, 240:304], ALU.mult, ALU.subtract), r=[srcR, extR, dR], w=[dR])
            A("act", actf(pst[:, 0:64], asb[:, 15 + AUX:15 + AUX + 64], AF.Copy), r=[asbR], w=[pstR])
            A("act", actf(pst[:, 64:79], asb[:, 15 + 1009:15 + 1024], AF.Copy), r=[asbR], w=[pstR])
            pb, pR = self.bank()
            A("pe", tr(pb[:, 0:128], pst, IDF[:]), r=[pstR, KR["ident_f"]], w=[pR])
            A("act", actf(po, pb[:, 0:128], AF.Copy), r=[pR], w=[poR])
            for tq in range(4):
                self.out_ops.append(A("sp", dma(self.pool_s[i, :, 11 + tq, fs], po[tq * 16:(tq + 1) * 16, :]), r=[poR], dma=True))
            self.out_ops.append(A("sp", dma(self.pool_p[i, :, fs], po[64:79, :]), r=[poR], dma=True))
            if j % 2 == 1:
                aoT, aoR = aoTs[g % 2]
                u2, u2R = self.ring_unit()
                gv = u2[:, 0:512].rearrange("p (k n) -> p k n", k=2)
                self.wload(u2[:, 0:512], W["wt_grp"][i, g], u2R)
                for m in range(2):
                    for (off, n) in GROUPS:
                        pb, pR = self.bank()
                        for kc in range(2):
                            A("pe", mm(pb[:, :n], gv[:, kc, m * 128:(m + 1) * 128], dT[:, kc, off:off + n], kc == 0, kc == 1), r=[dR, u2R], w=[pR])
                        A("act", actf(aoT[:, m, off:off + n], pb[:, :n], AF.Copy, scale=psc[:, g * 2 + m:g * 2 + m + 1]), r=[pR, pscR], w=[aoR])
                self.proj_out(False, [aoT[:, 0, :], aoT[:, 1, :]], [aoR, aoR],
                              [wout[(2 * g) * 128:(2 * g + 1) * 128, :], wout[(2 * g + 1) * 128:(2 * g + 2) * 128, :]])

    def ret_mixer(self, l):
        jj = l // 2
        A = self.A
        K, KR = self.K, self.KR
        W = self.W
        hT = self.hT
        allh = list(self.hT_R)
        IDB = K["ident_bf"]
        wq, wk = W["wt_q"][jj], W["wt_k"][jj]
        wo = W["w_ret_out"][jj]
        lg = _gammas().astype(np.float64)
        v_tm, vR = self.abuf(0, 2304, BF16, shape=[9, 512])
        kT, kTR = self.abuf(2304, 1152, BF16, shape=[2, NTOK])
        qT, qTR = self.abuf(3456, 1152, BF16, shape=[2, NTOK])
        ogT, ogTR = self.abuf(4608, 2304, BF16, shape=[4, NTOK])
        S, SR = self.abuf(6912, 1024, shape=[2, 512])
        Sb, SbR = self.abuf(7936, 512, BF16, shape=[2, 512])
        scT, scTR = self.abuf(8448, 64, BF16)
        qm, qmR = self.abuf(8512, 32, BF16)
        raw, rawR = self.abuf(8600, 1024, shape=[2, 512])
        r1, r1R = self.abuf(9624, 512)
        r2, r2R = self.abuf(10136, 512)
        gnb, gnbR = self.abuf(10648, 512)
        o_sb, oR = self.abuf(11160, 512)
        gs, gsR = self.abuf(11672, 512)
        og, ogR = self.abuf(12184, 256, BF16)
        ktm, ktmR = self.abuf(12440, 128, BF16)
        kzm, kzmR = self.abuf(12568, 128, BF16)
        A("dve", memset(o_sb, 0.0), w=[oR])
        raw2 = self.arena[:, 11160:12184].rearrange("p (k e) -> p k e", k=2)
        rawbufs = [(raw, [rawR]), (raw2, [oR, gsR])]
        rawi = [0]

        def kchunk_T(cols, zeta_ap, zR):
            pb, pR = self.bank()
            pT = pb.bitcast(BF16)
            for half in range(2):
                A("pe", tr(pT[:, half * 128:(half + 1) * 128], kT[:, half, cols], IDB[:]), r=[kTR, KR["ident_bf"]], w=[pR])
            A("dve", ts(ktm, pT[:, 0:256], zeta_ap, None, ALU.mult), r=[pR, zR], w=[ktmR])

        def finish_o(t, gunits):
            st6, mv, t1, rs, sR = self.stat()
            sub = self.stat_sub[id(sR)]
            A("dve", (lambda e, st6=st6: e.bn_stats(out=st6[:, 0, :], in_=o_sb)), r=[oR], w=[sub[0]])
            A("dve", (lambda e, st6=st6, mv=mv: e.bn_aggr(out=mv[:], in_=st6[:, 0, :])), r=[sub[0], sR], w=[sR])
            A("dve", ts(t1[:], mv[:, 1:2], EPS, None, ALU.add), r=[sR], w=[sR])
            A("act", actf(t1[:], t1[:], AF.Sqrt), r=[sR], w=[sR])
            A("dve", (lambda e, rs=rs, t1=t1: e.reciprocal(out=rs[:], in_=t1[:])), r=[sR], w=[sR])
            A("dve", stt(t1[:], mv[:, 0:1], -1.0, rs[:], ALU.mult, ALU.mult), r=[sR], w=[sR])
            A("dve", ts(o_sb, o_sb, rs[:, 0:1], t1[:, 0:1], ALU.mult, ALU.add), r=[oR, sR], w=[oR])
            A("dve", tt(o_sb, o_sb, gnb, ALU.mult), r=[oR, gnbR], w=[oR])
            pb, pR = self.bank()
            for k in range(16):
                uv, uR = gunits[k // 4]
                A("pe", mm(pb[:], hT[:, k, t * 128:(t + 1) * 128], uv[:, k % 4, :], k == 0, k == 15), r=[allh[t], uR], w=[pR])
            A("act", actf(gs, pb[:], AF.Silu), r=[pR], w=[gsR])
            A("dve", tt(og, o_sb, gs, ALU.mult), r=[oR, gsR], w=[ogR])
            pb, pR = self.bank()
            pT = pb.bitcast(BF16).rearrange("p (k c) -> p k c", k=8)
            for m in range(4):
                A("pe", tr(pT[:, m, :], og[:, m * 128:(m + 1) * 128], IDB[:]), r=[ogR, KR["ident_bf"]], w=[pR])
            A("act", actf(ogT[:, :, t * 128:(t + 1) * 128], pT[:, 0:4, :], AF.Copy), r=[pR], w=[ogTR])

        def state_update(c, gc, first_zero):
            for half in range(2):
                pb, pR = self.bank()
                A("pe", mm(pb[:], ktm[:, half * 128:(half + 1) * 128], v_tm[:, c, :], True, True), r=[ktmR, vR], w=[pR])
                if first_zero:
                    A("act", actf(S[:, half, :], pb[:], AF.Copy), r=[pR], w=[SR])
                else:
                    A("dve", stt(S[:, half, :], S[:, half, :], gc, pb[:], ALU.mult, ALU.add), r=[pR, SR], w=[SR])

        for h in range(HEADS):
            gam128 = float(np.exp(lg[h] * 128.0))
            gam4 = float(np.exp(lg[h] * 4.0))
            for (wsrc, dst, dstR, scale) in ((wq, qT, qTR, 1.0), (wk, kT, kTR, 1.0 / 16.0)):
                units = []
                for half in range(2):
                    u, uR = self.ring_unit()
                    uv = u.rearrange("p (k n) -> p k n", k=16)
                    self.wload(u, wsrc[h * 2 + half], uR)
                    units.append((uv, uR))
                for (off, n) in GROUPS:
                    hr = allh[off // 128:(off + n) // 128]
                    rbuf, rbR = rawbufs[rawi[0] % 2]
                    rawi[0] += 1
                    for half in range(2):
                        uv, uR = units[half]
                        pb, pR = self.bank()
                        for k in range(16):
                            A("pe", mm(pb[:, :n], uv[:, k, :], hT[:, k, off:off + n], k == 0, k == 15), r=hr + [uR], w=[pR])
                        A("act", actf(rbuf[:, half, 0:n], pb[:, :n], AF.Copy, scale=scale), r=[pR], w=rbR)
                    cs_, sn_ = K["cos"][:, off:off + n], K["sin"][:, off:off + n]
                    x1, x2 = rbuf[:, 0, 0:n], rbuf[:, 1, 0:n]
                    A("dve", tt(r1[:, 0:n], x1, cs_, ALU.mult), r=rbR + [KR["cos"]], w=[r1R])
                    A("dve", tt(r2[:, 0:n], x2, sn_, ALU.mult), r=rbR + [KR["sin"]], w=[r2R])
                    A("dve", tt(dst[:, 0, off:off + n], r1[:, 0:n], r2[:, 0:n], ALU.subtract), r=[r1R, r2R], w=[dstR])
                    A("dve", tt(r1[:, 0:n], x1, sn_, ALU.mult), r=rbR + [KR["sin"]], w=[r1R])
                    A("dve", tt(r2[:, 0:n], x2, cs_, ALU.mult), r=rbR + [KR["cos"]], w=[r2R])
                    A("dve", tt(dst[:, 1, off:off + n], r1[:, 0:n], r2[:, 0:n], ALU.add), r=[r1R, r2R], w=[dstR])
            vunits = []
            for kk in range(4):
                u, uR = self.ring_unit()
                uv = u.rearrange("p (k n) -> p k n", k=4)
                self.wload(u, W["wt_v"][jj, h, kk], uR)
                vunits.append((uv, uR))
            for t in range(NT):
                pb, pR = self.bank()
                for k in range(16):
                    uv, uR = vunits[k // 4]
                    A("pe", mm(pb[:], hT[:, k, t * 128:(t + 1) * 128], uv[:, k % 4, :], k == 0, k == 15), r=[allh[t], uR], w=[pR])
                A("act", actf(v_tm[:, t, :], pb[:], AF.Copy), r=[pR], w=[vR])
            for c in range(8):
                kchunk_T(slice(c * 128, (c + 1) * 128), K["zeta"][:, h:h + 1], KR["zeta"])
                state_update(c, gam128, c == 0)
            cin = self.nc.dram_tensor(f"ccs_in{jj}_{h}", [256, 512], F32).ap()
            cout = self.nc.dram_tensor(f"ccs_out{jj}_{h}", [512, 512], F32).ap()
            cR = Res()
            A("sp", dma(cin.rearrange("(k p) e -> p k e", p=128), S), r=[SR], w=[cR], dma=True)
            A("pool", lambda e, cin=cin, cout=cout: e.collective_compute(
                "AllGather", ALU.bypass, replica_groups=[[0, 1], [2, 3], [4, 5], [6, 7]],
                ins=[cin.opt()], outs=[cout.opt()]), r=[cR], w=[cR], dma="cc")
            gunits = []
            for kk in range(4):
                u, uR = self.ring_unit()
                uv = u.rearrange("p (k n) -> p k n", k=4)
                self.wload(u, W["wt_g"][jj, h, kk], uR)
                gunits.append((uv, uR))
            A("sp", dma(gnb, W["ret_norm_g"][jj, h].partition_broadcast(128)), w=[gnbR], dma=True)
            ac = slice(AUX, AUX + 128)
            pb, pR = self.bank()
            for half in range(2):
                A("pe", mm(pb[:, 0:128], kT[:, half, ac], qT[:, half, ac], half == 0, half == 1), r=[kTR, qTR], w=[pR])
            A("dve", tt(scT, pb[:, 0:128], K["dmask_aux"][:, h, :], ALU.mult), r=[pR, KR["dmask_aux"]], w=[scTR])
            pbi, pRi = self.bank()
            A("pe", mm(pbi[:], scT, v_tm[:, 8, :], True, True), r=[scTR, vR], w=[pRi])
            A("act", actf(o_sb, pbi[:], AF.Copy), r=[pRi], w=[oR])
            kchunk_T(ac, K["zeta_aux"][:, h:h + 1], KR["zeta_aux"])
            pbc, pRc = self.bank_ll()
            r12 = self.arena[:, 9624:10648].rearrange("p (k e) -> p k e", k=2)
            r12R = Res()
            r12R.r = [o for o in (r1R.r + r2R.r)] + [o for o in (r1R.w, r2R.w) if o is not None]
            stg = [(S, SR), (raw, rawR), (r12, r12R)]

            def load_state(s):
                b, bR = stg[s % 3]
                A("sp", dma(b, self.st_ret[jj, s, h].rearrange("(k p) e -> p k e", p=128)), w=[bR], dma=True)

            load_state(0)
            load_state(1)
            for s in range(16):
                Sx, SxR = stg[s % 3]
                A("act", actf(Sb, Sx, AF.Copy), r=[SxR], w=[SbR])
                for half in range(2):
                    A("dve", tt(qm, qT[:, half, AUX:AUX + 64], K["cmask"][:, s, :], ALU.mult), r=[qTR, KR["cmask"]], w=[qmR])
                    A("pe", mm(pbc[0:64, :], qm, Sb[:, half, :], (s == 0 and half == 0), (s == 15 and half == 1)), r=[qmR, SbR], w=[pRc])
                A("dve", ts(kzm, ktm, K["rmask"][:, s:s + 1], None, ALU.mult), r=[ktmR, KR["rmask"]], w=[kzmR])
                for half in range(2):
                    pb, pR = self.bank()
                    A("pe", mm(pb[:], kzm[:, half * 128:(half + 1) * 128], v_tm[:, 8, :], True, True), r=[kzmR, vR], w=[pR])
                    A("dve", stt(Sx[:, half, :], Sx[:, half, :], gam4, pb[:], ALU.mult, ALU.add), r=[pR, SxR], w=[SxR])
                self.out_ops.append(A("sp", dma(self.ret_s[jj, s, h].rearrange("(k p) e -> p k e", p=128), Sx), r=[SxR], dma=True))
                if s + 2 < 16:
                    load_state(s + 2)
            r1R.r = list(r12R.r) + ([r12R.w] if r12R.w is not None else [])
            r2R.r = list(r1R.r)
            r1R.w = None
            r2R.w = None
            A("dve", stt(o_sb[0:64, :], pbc[0:64, :], K["xi_aux"][0:64, h:h + 1], o_sb[0:64, :], ALU.mult, ALU.add), r=[pRc, KR["xi_aux"], oR], w=[oR])
            finish_o(8, gunits)
            A("sp", dma(S, cout[0:256, :].rearrange("(k p) e -> p k e", p=128)), r=[cR], w=[SR], dma=True)
            A("dve", ts(S, S, K["flag"][:, 0:1], None, ALU.mult), r=[SR, KR["flag"]], w=[SR])
            A("act", actf(Sb, S, AF.Copy), r=[SR], w=[SbR])
            for c in range(8):
                cc_ = slice(c * 128, (c + 1) * 128)
                pb, pR = self.bank()
                for half in range(2):
                    A("pe", mm(pb[:, 0:128], kT[:, half, cc_], qT[:, half, cc_], half == 0, half == 1), r=[kTR, qTR], w=[pR])
                A("dve", tt(scT, pb[:, 0:128], K["dmask"][:, h, :], ALU.mult), r=[pR, KR["dmask"]], w=[scTR])
                pbi, pRi = self.bank()
                A("pe", mm(pbi[:], scT, v_tm[:, c, :], True, True), r=[scTR, vR], w=[pRi])
                pbx, pRx = self.bank()
                for half in range(2):
                    A("pe", mm(pbx[:], qT[:, half, cc_], Sb[:, half, :], half == 0, half == 1), r=[qTR, SbR], w=[pRx])
                A("act", actf(o_sb, pbi[:], AF.Copy), r=[pRi], w=[oR])
                A("dve", stt(o_sb, pbx[:], K["xi"][:, h:h + 1], o_sb, ALU.mult, ALU.add), r=[pRx, KR["xi"], oR], w=[oR])
                kchunk_T(cc_, K["zeta"][:, h:h + 1], KR["zeta"])
                state_update(c, gam128, False)
                if c < 7:
                    A("act", actf(Sb, S, AF.Copy), r=[SR], w=[SbR])
                finish_o(c, gunits)
            self.out_ops.append(A("sp", dma(self.ret_p[jj, h].rearrange("(k p) e -> p k e", p=128), S), r=[SR], dma=True))
            for sl in range(2):
                r0 = h * 512 + sl * 256
                self.proj_out(h == 0 and sl == 0, [ogT[:, 2 * sl, :], ogT[:, 2 * sl + 1, :]], [ogTR, ogTR],
                              [wo[r0:r0 + 128, :], wo[r0 + 128:r0 + 256, :]])


    def build(self, mode="full"):
        gm_pre, gm_post = self.W["norm_mix_pre"], self.W["norm_mix_post"]
        gf_pre, gf_post = self.W["norm_ffn_pre"], self.W["norm_ffn_post"]
        L = self.n_layers
        if mode == "even_only":
            self.boundary(True, False, None, gm_pre[0])
            self.even_mixer(0)
            self.boundary(False, True, gm_post[0], None)
        elif mode == "ffn_only":
            self.boundary(True, False, None, gf_pre[0])
            self.ffn(0)
            self.boundary(False, True, gf_post[0], None)
        else:
            self.boundary(True, False, None, gm_pre[0])
            for l in range(L):
                if l % 2 == 0:
                    self.even_mixer(l)
                else:
                    self.ret_mixer(l)
                self.boundary(False, False, gm_post[l], gf_pre[l])
                self.ffn(l)
                last = (l == L - 1)
                self.boundary(False, last, gf_post[l], None if last else gm_pre[l + 1])
        self.A("sp", lambda e: None, extra=self.out_ops)
        self.S.finalize_and_emit()
        return self.nc


_PROG_CACHE = {}


def _get_prog(mode="full", n_layers=DEPTH):
    key = (mode, n_layers)
    if key not in _PROG_CACHE:
        p = Prog(n_layers)
        p.build(mode)
        _PROG_CACHE[key] = p.nc
    return _PROG_CACHE[key]


def make_in_maps(inputs):
    xp = np.asarray(inputs["x_prompt"], np.float32)
    xs = np.asarray(inputs["x_sample"], np.float32)
    maps = []
    wts = tile_weights(inputs)
    for c in range(8):
        p, half = c // 2, c % 2
        m = {}
        xin = np.zeros((NTOK, D), np.float32)
        xin[:1024] = xp[p, half * 1024:(half + 1) * 1024]
        xin[AUX:AUX + 64] = xs[16 * c:16 * c + 16].transpose(1, 0, 2).reshape(64, D)
        if half == 1:
            xin[AUX + 64:AUX + 96] = xp[p, 992:1024]
        m["x_in"] = xin
        m["st_pool"] = np.ascontiguousarray(inputs["state_pool"][:, 16 * c:16 * c + 16])
        m["st_ret"] = np.ascontiguousarray(inputs["state_ret"][:, 16 * c:16 * c + 16])
        m["st_conv"] = np.ascontiguousarray(inputs["state_conv"][:, 16 * c:16 * c + 16])
        for n, _ in WEIGHT_SPECS:
            m[n] = wts[n]
        for n, v in make_consts(half).items():
            m["c_" + n] = v
        maps.append(m)
    return maps


def assemble(res):
    R = res.results
    y_prompt = np.zeros((4, 2048, D), np.float32)
    y_sample = np.zeros((128, 4, D), np.float32)
    pool_prompt = np.zeros((2, 4, 15, 1024), np.float32)
    pool_sample = np.zeros((2, 128, 15, 1024), np.float32)
    chunk_v = np.zeros((2, 128, 4, 1024), np.float32)
    ret_prompt = np.zeros((2, 4, 8, 256, 512), np.float32)
    ret_sample = np.zeros((2, 128, 8, 256, 512), np.float32)
    conv_prompt = np.zeros((4, 4, 2, FF), np.float32)
    conv_sample = np.zeros((4, 128, 2, FF), np.float32)
    for c in range(8):
        p, half = c // 2, c % 2
        o = R[c]
        y_prompt[p, half * 1024:(half + 1) * 1024] = o["y_out"][:1024]
        y_sample[16 * c:16 * c + 16] = o["y_out"][AUX:AUX + 64].reshape(4, 16, D).transpose(1, 0, 2)
        pool_sample[:, 16 * c:16 * c + 16] = o["pool_s"]
        chunk_v[:, 16 * c:16 * c + 16] = o["cv_s"].reshape(2, 4, 16, 1024).transpose(0, 2, 1, 3)
        ret_sample[:, 16 * c:16 * c + 16] = o["ret_s"]
        conv_sample[:, 16 * c:16 * c + 16] = o["conv_s"]
        if half == 1:
            pool_prompt[:, p] = o["pool_p"]
            ret_prompt[:, p] = o["ret_p"]
            conv_prompt[:, p] = o["conv_p"]
    return (y_prompt, y_sample, pool_prompt, pool_sample, chunk_v, ret_prompt, ret_sample, conv_prompt, conv_sample)


def kernel(**inputs):
    nc = _get_prog()
    maps = make_in_maps(inputs)
    res = run_bass_kernel_spmd(nc, maps, core_ids=list(range(8)))
    return assemble(res)
```
